# Optimizing a Trainium2 kernel written in Bass

```python
import jax, jax.numpy as jnp
from jax import lax
import numpy as np

D_MODEL = 1024
BATCH = 2
SEQ = 8192
DEPTH = 2

EPS = 1e-6
NEG_INF = -1e30
CHUNK = 128
A_GROUPS = 4
A_WIDTH = D_MODEL
A_GROUP_DIM = A_WIDTH // A_GROUPS
B_WIDTH = D_MODEL
POOL_WINDOWS = (2, 4, 8, 16)
B_GROUPS = len(POOL_WINDOWS)
B_GROUP_DIM = B_WIDTH // B_GROUPS
MIX0_IN = 3 * A_WIDTH + 2 * B_WIDTH
MIX0_OUT = A_WIDTH + B_WIDTH
N_HEADS = 16
N_KV_HEADS = 4
HEAD_DIM = 64
GQA_GROUP = N_HEADS // N_KV_HEADS
C_WIDTH = N_HEADS * HEAD_DIM
KV_WIDTH = N_KV_HEADS * HEAD_DIM
MIX1_IN = 2 * C_WIDTH + 2 * KV_WIDTH
WINDOW = 128
ATTN_BLOCK = 128
ROPE_THETA = 500000.0
ROT_DIM = HEAD_DIM // 4

kernel_name = "hybrid_gmlp_pool_swa_encoder"


def rms_norm(x, g):
    xf = x.astype(jnp.float32)
    y = xf * lax.rsqrt(jnp.mean(xf * xf, axis=-1, keepdims=True) + EPS)
    return (y * g.astype(jnp.float32)).astype(x.dtype)


def spatial_gating_mixer(u, v, w_s, b_s, g_v):
    bn, s, _ = u.shape
    u = jax.nn.gelu(u)
    v = rms_norm(jax.nn.gelu(v), g_v)
    v = v.reshape(bn, s // CHUNK, CHUNK, A_GROUPS, A_GROUP_DIM)
    mixed = jnp.einsum('hij,bcjhd->bcihd', w_s, v) + b_s.T[None, None, :, :, None]
    return u * mixed.reshape(bn, s, A_WIDTH)


def multiscale_pool_mixer(xb, w_g, scale):
    bn, s, _ = xb.shape
    xf = xb.astype(jnp.float32)
    cs = jnp.concatenate([jnp.zeros((bn, 1, B_WIDTH), jnp.float32), lax.cumsum(xf, axis=1)], axis=1)
    t = jnp.arange(s)
    outs = []
    for gi, w in enumerate(POOL_WINDOWS):
        lo = jnp.clip(t - w // 2, 0, s)
        hi = jnp.clip(t + w // 2, 0, s)
        sl = slice(gi * B_GROUP_DIM, (gi + 1) * B_GROUP_DIM)
        csg = cs[:, :, sl]
        win_sum = jnp.take(csg, hi, axis=1) - jnp.take(csg, lo, axis=1)
        mean = win_sum / (hi - lo).astype(jnp.float32)[None, :, None]
        outs.append(mean - xf[:, :, sl])
    p = jnp.stack(outs, axis=2).astype(xb.dtype)
    y = jnp.einsum('bsgd,gde->bsge', p, w_g).reshape(bn, s, B_WIDTH)
    return y * scale


def partial_rope(x, pos):
    inv = ROPE_THETA ** (-jnp.arange(0, ROT_DIM, 2, dtype=jnp.float32) / ROT_DIM)
    ang = pos.astype(jnp.float32)[:, None] * inv[None, :]
    cos = jnp.cos(ang)[None, :, None, :]
    sin = jnp.sin(ang)[None, :, None, :]
    xr = x[..., :ROT_DIM].astype(jnp.float32)
    x1, x2 = xr[..., :ROT_DIM // 2], xr[..., ROT_DIM // 2:]
    rot = jnp.concatenate([x1 * cos - x2 * sin, x2 * cos + x1 * sin], axis=-1)
    return jnp.concatenate([rot.astype(x.dtype), x[..., ROT_DIM:]], axis=-1)


def windowed_gqa(q, k, v, sink):
    bn, s = q.shape[:2]
    nb = s // ATTN_BLOCK
    qb = q.reshape(bn, nb, ATTN_BLOCK, N_KV_HEADS, GQA_GROUP, HEAD_DIM)

    def band(t):
        tp = jnp.pad(t, ((0, 0), (ATTN_BLOCK, ATTN_BLOCK), (0, 0), (0, 0)))
        tp = tp.reshape(bn, nb + 2, ATTN_BLOCK, N_KV_HEADS, HEAD_DIM)
        return jnp.concatenate([tp[:, :nb], tp[:, 1:nb + 1], tp[:, 2:]], axis=2)

    kb, vb = band(k), band(v)
    scores = jnp.einsum('bnqkgd,bnskd->bnkgqs', qb, kb).astype(jnp.float32) * (HEAD_DIM ** -0.5)
    qi = jnp.arange(nb)[:, None, None] * ATTN_BLOCK + jnp.arange(ATTN_BLOCK)[None, :, None]
    kj = (jnp.arange(nb)[:, None, None] - 1) * ATTN_BLOCK + jnp.arange(3 * ATTN_BLOCK)[None, None, :]
    allowed = (jnp.abs(qi - kj) <= WINDOW) & (kj >= 0) & (kj < s)
    scores = jnp.where(allowed[None, :, None, None], scores, NEG_INF)
    sink_col = jnp.broadcast_to(sink.astype(jnp.float32).reshape(1, 1, N_KV_HEADS, GQA_GROUP, 1, 1),
                                scores.shape[:-1] + (1,))
    probs = jax.nn.softmax(jnp.concatenate([scores, sink_col], axis=-1), axis=-1)[..., :-1]
    out = jnp.einsum('bnkgqs,bnskd->bnqkgd', probs.astype(v.dtype), vb)
    return out.reshape(bn, s, C_WIDTH)


def setup_inputs(seed: int = 0) -> dict:
    key = jax.random.key(seed)
    ks = jax.random.split(key, 16)
    f32 = jnp.float32
    nrm = lambda k, shp, sc: jax.random.normal(k, shp, f32) * sc
    return {
        "x": nrm(ks[0], (BATCH, SEQ, D_MODEL), 1.0),
        "norm_0": 1.0 + nrm(ks[1], (D_MODEL,), 0.05),
        "w_in_0": nrm(ks[2], (D_MODEL, MIX0_IN), D_MODEL ** -0.5),
        "a_v_norm_0": 1.0 + nrm(ks[3], (A_WIDTH,), 0.05),
        "a_spatial_w_0": nrm(ks[4], (A_GROUPS, CHUNK, CHUNK), CHUNK ** -0.5),
        "a_spatial_b_0": 1.0 + nrm(ks[5], (A_GROUPS, CHUNK), 0.1),
        "b_group_w_0": nrm(ks[6], (B_GROUPS, B_GROUP_DIM, B_GROUP_DIM), B_GROUP_DIM ** -0.5),
        "b_scale_0": 1.0 + nrm(ks[7], (B_WIDTH,), 0.05),
        "w_out_0": nrm(ks[8], (MIX0_OUT, D_MODEL), MIX0_OUT ** -0.5),
        "norm_1": 1.0 + nrm(ks[9], (D_MODEL,), 0.05),
        "w_in_1": nrm(ks[10], (D_MODEL, MIX1_IN), D_MODEL ** -0.5),
        "sink_1": nrm(ks[11], (N_HEADS,), 0.5),
        "w_out_1": nrm(ks[12], (C_WIDTH, D_MODEL), C_WIDTH ** -0.5),
        "final_norm": 1.0 + nrm(ks[13], (D_MODEL,), 0.05),
    }


def reference(x, norm_0, w_in_0, a_v_norm_0, a_spatial_w_0, a_spatial_b_0, b_group_w_0, b_scale_0,
              w_out_0, norm_1, w_in_1, sink_1, w_out_1, final_norm):
    bn, s, _ = x.shape
    pos = jnp.arange(s)
    for layer in range(DEPTH):
        if layer % 2 == 0:
            h = rms_norm(x, norm_0)
            z = h @ w_in_0
            a_u, a_v, a_gate, b_x, b_gate = jnp.split(
                z, [A_WIDTH, 2 * A_WIDTH, 3 * A_WIDTH, 3 * A_WIDTH + B_WIDTH], axis=-1)
            ya = spatial_gating_mixer(a_u, a_v, a_spatial_w_0, a_spatial_b_0, a_v_norm_0) * jax.nn.silu(a_gate)
            yb = multiscale_pool_mixer(b_x, b_group_w_0, b_scale_0) * jax.nn.silu(b_gate)
            x = x + jnp.concatenate([ya, yb], axis=-1) @ w_out_0
        else:
            h = rms_norm(x, norm_1)
            z = h @ w_in_1
            q, k, v, gate = jnp.split(z, [C_WIDTH, C_WIDTH + KV_WIDTH, C_WIDTH + 2 * KV_WIDTH], axis=-1)
            q = partial_rope(q.reshape(bn, s, N_HEADS, HEAD_DIM), pos)
            k = partial_rope(k.reshape(bn, s, N_KV_HEADS, HEAD_DIM), pos)
            v = v.reshape(bn, s, N_KV_HEADS, HEAD_DIM)
            y = windowed_gqa(q, k, v, sink_1) * jax.nn.silu(gate)
            x = x + y @ w_out_1
    return rms_norm(x, final_norm)
```

```python
import numpy as np
import concourse.bass as bass
import concourse.mybir as mybir
from concourse.bass_utils import run_bass_kernel_spmd
from concourse.alu_op_type import AluOpType as ALU

AF = mybir.ActivationFunctionType
F32 = mybir.dt.float32
BF16 = mybir.dt.bfloat16

SBUF_LO = 16512
SBUF_HI = 229344

D = 1024
S_LEN = 8192
NT = 20
EPS = 1e-6
N_CORES = 8
QPAIR_HA = [0, 1, 2, 3, 8, 9, 10, 11]


class Sched:
    def __init__(self, nc):
        self.nc = nc
        self.names = ['pe', 'act', 'dve', 'pool', 'sp']
        self.prog = {e: [] for e in self.names}
        self.cnt = {e: 0 for e in self.names}
        self.sems = {}
        self.known = {e: {} for e in self.names}
        self.lastw = {}
        self.readers = {}
        self.dcount = {}
        self.dkeys = {}
        self.nwaits = 0

    def sem(self, sk):
        if sk not in self.sems:
            self.sems[sk] = self.nc.alloc_semaphore(name=sk.replace(':', '_'))
        return self.sems[sk]

    def _waits(self, eng, reads, writes):
        deps = {}

        def add(sk, v, raw):
            if sk == 'E:' + eng and eng == 'pe':
                return
            if deps.get(sk, 0) < v:
                deps[sk] = v

        for k in reads:
            if k in self.lastw:
                add(*self.lastw[k], True)
            if k[:2] in ('pb', 'pt'):
                for sk, v in self.readers.get(k, {}).items():
                    add(sk, v, False)
        for k in writes:
            if k in self.lastw:
                add(*self.lastw[k], False)
            for sk, v in self.readers.get(k, {}).items():
                add(sk, v, False)
        waits = []
        for sk, v in deps.items():
            if self.known[eng].get(sk, 0) >= v:
                continue
            self.known[eng][sk] = v
            waits.append((sk, v))
            self.sem(sk)
        self.nwaits += len(waits)
        return waits

    def _record(self, me, reads, writes):
        for k in reads:
            self.readers.setdefault(k, {})[me[0]] = me[1]
        for k in writes:
            self.lastw[k] = me
            self.readers[k] = {}

    def op(self, eng, fn, reads=(), writes=(), signal=True):
        waits = self._waits(eng, reads, writes)
        sk = 'E:' + eng
        self.sem(sk)
        if signal:
            self.cnt[eng] += 1
            me = (sk, self.cnt[eng])
        else:
            me = (sk, self.cnt[eng] + 1)
        self.prog[eng].append((waits, fn, me if signal else None))
        self._record(me, reads, writes)

    def dma(self, queue, out, in_, reads=(), writes=(), sem=None):
        waits = self._waits(queue, reads, writes)
        sk = 'D:' + sem
        self.sem(sk)
        self.dcount[sk] = self.dcount.get(sk, 0) + 16
        me = (sk, self.dcount[sk])
        self.prog[queue].append((waits, lambda e: e.dma_start(out=out, in_=in_), me))
        self._record(me, reads, writes)
        ks = self.dkeys.setdefault(sk, set())
        ks.update(writes)
        for k in ks:
            if k in self.lastw and self.lastw[k][0] == sk:
                self.lastw[k] = me

    def wait_all(self, eng, keys):
        waits = self._waits(eng, keys, ())
        self.prog[eng].append((waits, None, None))

    def barrier(self):
        tot = {}
        for e in self.names:
            if self.cnt[e]:
                tot['E:' + e] = self.cnt[e]
        for sk, v in self.dcount.items():
            tot[sk] = v
        for e in self.names:
            waits = []
            for sk, v in tot.items():
                if sk == 'E:' + e:
                    continue
                if self.known[e].get(sk, 0) >= v:
                    continue
                self.known[e][sk] = v
                waits.append((sk, v))
            self.prog[e].append((waits, None, None))
        self.lastw = {}
        self.readers = {}

    def emit(self, block):
        deco = {'pe': block.tensor, 'act': block.scalar, 'dve': block.vector,
                'pool': block.gpsimd, 'sp': block.sync}
        for e in self.names:
            prog = self.prog[e]

            def body(eng, prog=prog):
                for waits, fn, me in prog:
                    for sk, v in waits:
                        eng.wait_ge(self.sems[sk], v)
                    if fn is not None:
                        inst = fn(eng)
                        if me is not None:
                            inst.then_inc(self.sems[me[0]], 16 if me[0][0] == 'D' else 1)

            deco[e](body)

    def mm(self, out, lhsT, rhs, start, stop, reads, writes, signal=None):
        if signal is None:
            signal = stop
        self.op('pe', lambda e: e.matmul(out, lhsT, rhs, start=start, stop=stop), reads, writes, signal)

    def tr(self, out, in_, ident, reads, writes, signal=True):
        self.op('pe', lambda e: e.transpose(out, in_, ident), list(reads) + ['ident'], writes, signal)

    def act(self, out, in_, func, reads, writes, **kw):
        self.op('act', lambda e: e.activation(out=out, in_=in_, func=func, **kw), reads, writes)

    def copy(self, eng, out, in_, reads, writes):
        if eng == 'act':
            self.op('act', lambda e: e.activation(out=out, in_=in_, func=AF.Copy), reads, writes)
        else:
            self.op(eng, lambda e: e.tensor_copy(out=out, in_=in_), reads, writes)

    def tt(self, eng, out, in0, in1, op, reads, writes):
        self.op(eng, lambda e: e.tensor_tensor(out=out, in0=in0, in1=in1, op=op), reads, writes)

    def ts(self, eng, out, in0, s1, s2, op0, op1, reads, writes):
        if op1 is None:
            self.op(eng, lambda e: e.tensor_scalar(out=out, in0=in0, scalar1=s1, scalar2=None, op0=op0),
                    reads, writes)
        else:
            self.op(eng, lambda e: e.tensor_scalar(out=out, in0=in0, scalar1=s1, scalar2=s2, op0=op0, op1=op1),
                    reads, writes)

    def stt(self, out, in0, scalar, in1, op0, op1, reads, writes):
        self.op('dve', lambda e: e.scalar_tensor_tensor(out=out, in0=in0, scalar=scalar, in1=in1,
                                                        op0=op0, op1=op1), reads, writes)


class Arena:
    def __init__(self, nc):
        self.nc = nc
        self.off = SBUF_LO
        self.n = 0
        self.peak = 0

    def alloc(self, name, shape, dtype):
        esz = 4 if dtype == F32 else 2
        nbytes = int(np.prod(shape[1:])) * esz
        nbytes = (nbytes + 63) // 64 * 64
        assert self.off + nbytes <= SBUF_HI, f"SBUF overflow allocating {name}: {self.off + nbytes - SBUF_HI}"
        t = self.nc.alloc_sbuf_tensor_at(f"{name}_{self.n}", list(shape), dtype, offset=self.off)
        self.n += 1
        self.off += nbytes
        self.peak = max(self.peak, self.off)
        return t

    def mark(self):
        return self.off

    def reset(self, m):
        self.off = m


def build(stop_after=None):
    nc = bass.Bass("TRN2", target_bir_lowering=False)
    S = Sched(nc)
    A = Arena(nc)

    def din(name, shape):
        return nc.dram_tensor(name, list(shape), F32, kind="ExternalInput").ap()

    xp = din("xp", [NT * 128, D])
    w_in_0 = din("w_in_0", [1024, 5120]).rearrange("(kc p) n -> p kc n", p=128)
    w_out_0 = din("w_out_0", [2048, 1024]).rearrange("(kc p) n -> p kc n", p=128)
    w_in_1 = din("w_in_1", [1024, 2560]).rearrange("(kc p) n -> p kc n", p=128)
    w_out_1 = din("w_out_1", [1024, 1024]).rearrange("(kc p) n -> p kc n", p=128)
    d_g0 = din("g0_bc", [128, D])
    d_g1 = din("g1_bc", [128, D])
    d_gf = din("gf_bc", [128, D])
    d_ident = din("ident", [128, 128])
    d_wsT = din("wsT", [128, 4, 128])
    d_bbc = din("b_bc", [128, 4, 128])
    d_gv = din("gv_col", [128, 8])
    d_bsc = din("bsc_col", [128, 8])
    d_wg = din("wg", [128, 4, 2, 256])
    d_rp = din("rpool", [128, 5, 4, 128])
    d_mk = din("masks", [128, 3, 2, 128])
    d_cos = din("cos", [128, 18, 8])
    d_sin = din("sin", [128, 18, 8])
    d_sink = din("sink_bc", [128, 16])
    out = nc.dram_tensor("out", [16 * 128, D], F32, kind="ExternalOutput").ap()
    dbg = None
    if stop_after:
        dbg = nc.dram_tensor("dbg", [18 * 128, D], F32, kind="ExternalOutput").ap()

    X1 = A.alloc("X1", [128, 18, D], F32)
    G0 = A.alloc("G0", [128, D], F32)
    HB = [None, None, None]
    IDENT = A.alloc("IDENT", [128, 128], BF16)
    MHALF = A.alloc("MHALF", [128, 1], F32)
    ST = A.alloc("ST", [128, 128], F32)
    JUNK = A.alloc("JUNK", [128, D], BF16)

    PT = [nc.alloc_psum_tensor(f"pt{i}", [128, 8, 128], BF16) for i in range(2)]
    PB = [nc.alloc_psum_tensor(f"pb{i}", [128, 512], F32) for i in range(6)]
    rr = {'pt': 0, 'pb': 0, 'st': 0, 'hb': 0}

    def next_pt():
        i = rr['pt'] % 2
        rr['pt'] += 1
        return PT[i], f'pt{i}'

    def next_pb():
        i = rr['pb'] % 6
        rr['pb'] += 1
        return PB[i], f'pb{i}'

    def new_stat():
        i = rr['st']
        rr['st'] += 1
        assert i < 128
        return ST[:, i:i + 1], f'st{i}'

    S.dma('pool', IDENT[:], d_ident[:, :], writes=['ident'], sem='c_ident')
    S.op('pool', lambda e: e.memset(MHALF[:], -0.5), writes=['mhalf'])

    def rstd_from_sumsq(st, stk):
        S.ts('pool', st, st, 1.0 / D, EPS, ALU.mult, ALU.add, [stk], [stk])
        S.tt('pool', st, st, MHALF[:], ALU.pow, [stk, 'mhalf'], [stk])

    def sumsq(x_ap, xkey):
        st, stk = new_stat()
        S.act(JUNK[:], x_ap, AF.Square, [xkey], ['junk', stk], accum_out=st)
        return st, stk

    def norm_stats(x_ap, xkey):
        st, stk = sumsq(x_ap, xkey)
        rstd_from_sumsq(st, stk)
        return st, stk

    def norm_apply(x_ap, xkey, gain, gkey, i, st, stk):
        S.stt(HB[i][:], x_ap, st, gain, ALU.mult, ALU.mult, [xkey, stk, gkey], [f'hb{i}'])

    def norm_h(x_ap, xkey, gain, gkey, i):
        st, stk = norm_stats(x_ap, xkey)
        norm_apply(x_ap, xkey, gain, gkey, i, st, stk)

    def h_to_hT(i, dst_ap, dst_key, evac):
        hb, hkey = HB[i], f'hb{i}'
        pt, ptk = next_pt()
        for c in range(8):
            S.tr(pt[:, c, :], hb[:, c * 128:(c + 1) * 128], IDENT[:], [hkey], [ptk], signal=(c == 7))
        S.copy(evac, dst_ap, pt[:, :, :], [ptk], [dst_key])

    def load_weights(WA, w_in, c0, parts, WO, w_out, k0, hook=None):
        a, b = parts[0]
        for kc in range(8):
            S.dma('pool', WA[:, kc, a:b], w_in[:, kc, c0 + a:c0 + b],
                  writes=[f'WA{kc}_{blk}' for blk in range(a // 512, b // 512)], sem=f'WA{kc}p0')
        if hook is not None:
            hook()
        for pi, (a, b) in enumerate(parts[1:]):
            S.dma('pool', WA[:, :, a:b], w_in[:, :, c0 + a:c0 + b],
                  writes=[f'WA{kc}_{blk}' for kc in range(8) for blk in range(a // 512, b // 512)],
                  sem=f'WAp{pi + 1}')
        S.dma('pool', WO[:, :, :], w_out[:, k0:k0 + 8, :], writes=[f'WO{kc}' for kc in range(8)], sem='WOall')

    def out_proj_resid(Y, ykey, t, WO, dst_ap_fn, dst_key, res_ap_fn, res_key):
        for nb in range(2):
            po, pok = next_pb()
            for c in range(8):
                S.mm(po[:, :], Y[:, c, t * 128:(t + 1) * 128], WO[:, c, nb * 512:(nb + 1) * 512],
                     c == 0, c == 7, [ykey, f'WO{c}'], [pok])
            S.tt('dve', dst_ap_fn(nb), po[:, :], res_ap_fn(nb), ALU.add, [pok, res_key], [dst_key])

    phase_mark = A.mark()
    groups = [[1 + 3 * G + t for t in range(3)] for G in range(6)]

    for _i in range(3):
        HB[_i] = A.alloc("HB", [128, D], BF16)
    WA = A.alloc("WA", [128, 8, 3072], BF16)
    WO = A.alloc("WO", [128, 8, 1024], BF16)
    WST = A.alloc("WST", [128, 4, 128], BF16)
    BBC = A.alloc("BBC", [128, 4, 128], F32)
    GV = A.alloc("GV", [128, 8], F32)
    HT = [A.alloc("HT", [128, 8, 384], BF16) for _ in range(2)]
    VB = [A.alloc("VB", [128, D], BF16) for _ in range(3)]
    WS = [A.alloc("WS", [128, 4, 128], BF16) for _ in range(3)]
    UC = [A.alloc("UC", [128, 384], F32) for _ in range(2)]
    TG = [A.alloc("TG", [128, 384], F32) for _ in range(2)]
    M1 = [A.alloc("M1", [128, 384], F32) for _ in range(2)]
    YA = [A.alloc("YA", [128, 8, 384], BF16) for _ in range(2)]

    for T in groups[0]:
        S.dma('sp', X1[:, T - 1, :], xp[T * 128:(T + 1) * 128, :], writes=[f'x1_{T}'], sem='x1g0')
    S.dma('sp', G0[:], d_g0[:, :], writes=['g0'], sem='c_g0')
    S.dma('pool', WST[:], d_wsT[:, :, :], writes=['wst'], sem='c_wst')
    S.dma('sp', BBC[:], d_bbc[:, :, :], writes=['bbc'], sem='c_bbc')
    S.dma('sp', GV[:], d_gv[:, :], writes=['gv'], sem='c_gv')

    SKIPA = False
    if SKIPA:
        for T in range(1, 19):
            S.dma('sp', X1[:, T - 1, :], xp[T * 128:(T + 1) * 128, :], writes=[f'x1_{T}'], sem='x1all')
    a1_st = {}

    def a1_front_s(G):
        for t, T in enumerate(groups[G]):
            if G > 0:
                S.dma('sp', X1[:, T - 1, :], xp[T * 128:(T + 1) * 128, :], writes=[f'x1_{T}'], sem=f'x1g{G}')
        for t, T in enumerate(groups[G]):
            a1_st[T] = norm_stats(X1[:, T - 1, :], f'x1_{T}')

    def a1_front_h(G, only=None):
        for t, T in enumerate(groups[G]):
            if only is None or only == t:
                norm_apply(X1[:, T - 1, :], f'x1_{T}', G0[:], 'g0', t, *a1_st[T])

    def a1_front_a(G):
        a1_front_s(G)
        a1_front_h(G)

    def a1_front_b(G):
        HTg, htk = HT[G % 2], f'ht{G % 2}'
        for t, T in enumerate(groups[G]):
            h_to_hT(t, HTg[:, :, t * 128:(t + 1) * 128], htk + f'_{t}', 'act')

    def a1_mid_v(G):
        Ts = groups[G]
        HTg, htk = HT[G % 2], f'ht{G % 2}'
        for t, T in enumerate(Ts):
            for nb in range(2):
                pv, pvk = next_pb()
                for kc in range(8):
                    S.mm(pv[:, :], HTg[:, kc, t * 128:(t + 1) * 128],
                         WA[:, kc, 1024 + nb * 512:1024 + (nb + 1) * 512], kc == 0, kc == 7,
                         [htk + f'_{t}', f'WA{kc}_{2 + nb}'], [pvk])
                S.act(VB[t][:, nb * 512:(nb + 1) * 512], pv[:, :], AF.Gelu_apprx_tanh, [pvk], [f'vb{t}'])
            st, stk = sumsq(VB[t][:], f'vb{t}')
            rstd_from_sumsq(st, stk)
            S.ts('dve', WS[t][:], WST[:], st, None, ALU.mult, None, ['wst', stk], [f'ws{t}'])

    def a1_mid_c(G):
        HTg, htk = HT[G % 2], f'ht{G % 2}'
        YAg, yak = YA[G % 2], f'ya{G % 2}'
        for c in range(8):
            if c == 1 and G + 1 < 6:
                a1_front_s(G + 1)
            if c in (3, 4, 5) and G + 1 < 6:
                a1_front_h(G + 1, c - 3)
            hd = c // 2
            uc, uck = UC[c % 2], f'uc{c % 2}'
            tg, tgk = TG[c % 2], f'tg{c % 2}'
            m1, m1k = M1[c % 2], f'm1{c % 2}'
            pu, puk = next_pb()
            for kc in range(8):
                S.mm(pu[:, 0:384], WA[:, kc, c * 128:(c + 1) * 128], HTg[:, kc, :], kc == 0, kc == 7,
                     [htk + '_0', htk + '_1', htk + '_2', f'WA{kc}_{c // 4}'], [puk])
            S.act(uc[:], pu[:, 0:384], AF.Gelu_apprx_tanh, [puk], [uck])
            pg, pgk = next_pb()
            for kc in range(8):
                S.mm(pg[:, 0:384], WA[:, kc, 2048 + c * 128:2048 + (c + 1) * 128], HTg[:, kc, :],
                     kc == 0, kc == 7, [htk + '_0', htk + '_1', htk + '_2', f'WA{kc}_{4 + c // 4}'], [pgk])
            S.act(tg[:], pg[:, 0:384], AF.Tanh, [pgk], [tgk], scale=0.5)
            S.stt(tg[:], tg[:], 1.0, pg[:, 0:384], ALU.add, ALU.mult, [tgk, pgk], [tgk])
            psp, pspk = next_pb()
            for t in range(3):
                S.mm(psp[:, t * 128:(t + 1) * 128], VB[t][:, c * 128:(c + 1) * 128], WS[t][:, hd, :],
                     True, True, [f'vb{t}', f'ws{t}'], [pspk], signal=(t == 2))
            S.stt(m1[:].rearrange("p (t i) -> p t i", t=3), psp[:, 0:384].rearrange("p (t i) -> p t i", t=3),
                  GV[:, c:c + 1], BBC[:, hd:hd + 1, :].broadcast_to([128, 3, 128]), ALU.mult, ALU.add,
                  [pspk, 'gv', 'bbc'], [m1k])
            S.tt('pool', m1[:], m1[:], uc[:], ALU.mult, [m1k, uck], [m1k])
            S.stt(YAg[:, c, :], m1[:], 0.5, tg[:], ALU.mult, ALU.mult, [m1k, tgk], [yak])

    def a1_back(G):
        YAg, yak = YA[G % 2], f'ya{G % 2}'
        for t, T in enumerate(groups[G]):
            out_proj_resid(YAg, yak, t, WO,
                           lambda nb, T=T: X1[:, T - 1, nb * 512:(nb + 1) * 512], f'x1_{T}',
                           lambda nb, T=T: X1[:, T - 1, nb * 512:(nb + 1) * 512], f'x1_{T}')

    load_weights(WA, w_in_0, 0, [(1024, 1536), (1536, 2048), (0, 512), (2048, 2560), (512, 1024), (2560, 3072)], WO, w_out_0, 0,
                 hook=(None if SKIPA else (lambda: a1_front_a(0))))
    if not SKIPA:
        a1_front_b(0)
        a1_mid_v(0)
        for G in range(6):
            a1_mid_c(G)
            if G + 1 < 6:
                a1_front_b(G + 1)
                a1_mid_v(G + 1)
            a1_back(G)

    def dump_x1_and_finish():
        S.barrier()
        for T in range(1, 19):
            S.dma('sp', dbg[(T - 1) * 128:T * 128, :], X1[:, T - 1, :], writes=[f'dbg{T}'], sem='dbg')
        S.wait_all('sp', [f'dbg{T}' for T in range(1, 19)])

    if stop_after == 'A1':
        dump_x1_and_finish()
        return finish(nc, S, A)

    S.barrier()
    A.reset(phase_mark)
    for _i in range(3):
        HB[_i] = A.alloc("HB", [128, D], BF16)
    WA = A.alloc("WA", [128, 8, 2048], BF16)
    WO = A.alloc("WO", [128, 8, 1024], BF16)
    WG = A.alloc("WG", [128, 4, 2, 256], BF16)
    RP = A.alloc("RP", [128, 5, 4, 128], BF16)
    BSC = A.alloc("BSC", [128, 8], F32)
    XIN = [A.alloc("XIN", [128, D], F32) for _ in range(3)]
    HT = [A.alloc("HT", [128, 8, 384], BF16) for _ in range(2)]
    BX = [A.alloc("BX", [128, D], BF16) for _ in range(7)]
    a2_hook = [None]
    SGB = [A.alloc("SGB", [128, 8, 384], BF16) for _ in range(2)]
    TGC = [A.alloc("TGC", [128, 384], F32) for _ in range(2)]
    PP = A.alloc("PP", [128, 8, 384], BF16)
    YB = A.alloc("YB", [128, 8, 384], BF16)

    S.dma('sp', BSC[:], d_bsc[:, :], writes=['bsc'], sem='c_bsc')
    S.ts('dve', BSC[:], BSC[:], 0.5, None, ALU.mult, None, ['bsc'], ['bsc'])
    xin_n = [0]

    a2_st = {}

    def a2_fs(Ts):
        for t, T in enumerate(Ts):
            xi = xin_n[0] % 3
            xin_n[0] += 1
            S.dma('sp', XIN[xi][:], xp[T * 128:(T + 1) * 128, :], writes=[f'xin{xi}'], sem=f'xin{xi}')
            a2_st[T] = (xi,) + norm_stats(XIN[xi][:], f'xin{xi}')

    def a2_fh(Ts, only=None):
        for t, T in enumerate(Ts):
            if only is None or only == t:
                xi, st, stk = a2_st[T]
                norm_apply(XIN[xi][:], f'xin{xi}', G0[:], 'g0', t, st, stk)

    def a2_fa(Ts):
        a2_fs(Ts)
        a2_fh(Ts)

    def a2_fb(Ts, slot):
        HTg, htk = HT[slot], f'ht{slot}'
        for t, T in enumerate(Ts):
            h_to_hT(t, HTg[:, :, t * 128:(t + 1) * 128], htk + f'_{t}', 'act')

    def a2_x(Ts, slot, full):
        HTg, htk = HT[slot], f'ht{slot}'
        for t, T in enumerate(Ts):
            for nb in range(2):
                pb, pbk = next_pb()
                for kc in range(8):
                    S.mm(pb[:, :], HTg[:, kc, t * 128:(t + 1) * 128], WA[:, kc, nb * 512:(nb + 1) * 512],
                         kc == 0, kc == 7, [htk + f'_{t}', f'WA{kc}_{nb}'], [pbk])
                S.copy('act' if nb else 'dve', BX[T % 7][:, nb * 512:(nb + 1) * 512], pb[:, :],
                       [pbk], [f'bx{T % 7}'])
        if not full:
            return
        SGg, sgk = SGB[slot], f'sgb{slot}'
        for c in range(8):
            if c in (3, 4, 5) and a2_hook[0] is not None:
                a2_fh(a2_hook[0], c - 3)
                if c == 5:
                    a2_hook[0] = None
            tgc, tgck = TGC[c % 2], f'tgc{c % 2}'
            pg, pgk = next_pb()
            for kc in range(8):
                S.mm(pg[:, 0:384], WA[:, kc, 1024 + c * 128:1024 + (c + 1) * 128], HTg[:, kc, :],
                     kc == 0, kc == 7, [htk + '_0', htk + '_1', htk + '_2', f'WA{kc}_{2 + c // 4}'], [pgk])
            S.act(tgc[:], pg[:, 0:384], AF.Tanh, [pgk], [tgck], scale=0.5)
            S.stt(SGg[:, c, :], tgc[:], 1.0, pg[:, 0:384], ALU.add, ALU.mult, [tgck, pgk], [sgk])

    def a2_p(Ts, slot):
        SGg, sgk = SGB[slot], f'sgb{slot}'
        for c in range(8):
            g = c // 2
            pp, ppk = next_pb()
            for t, T in enumerate(Ts):
                cur = 3 if T == 2 else (4 if T == 17 else 1)
                cs = slice(c * 128, (c + 1) * 128)
                o = pp[:, t * 128:(t + 1) * 128]
                S.mm(o, BX[(T - 1) % 7][:, cs], RP[:, 0, g, :], True, False, [f'bx{(T - 1) % 7}', 'rp'], [ppk])
                S.mm(o, BX[T % 7][:, cs], RP[:, cur, g, :], False, False, [f'bx{T % 7}', 'rp'], [ppk])
                S.mm(o, BX[(T + 1) % 7][:, cs], RP[:, 2, g, :], False, True, [f'bx{(T + 1) % 7}', 'rp'], [ppk],
                     signal=(t == 2))
            S.copy('act', PP[:, c, :], pp[:, 0:384], [ppk], ['pp'])
        for c in range(8):
            g, ec = c // 2, c % 2
            py, pyk = next_pb()
            for dc in range(2):
                S.mm(py[:, 0:384], WG[:, g, dc, ec * 128:(ec + 1) * 128], PP[:, g * 2 + dc, :],
                     dc == 0, dc == 1, ['wg', 'pp'], [pyk])
            S.stt(YB[:, c, :], py[:, 0:384], BSC[:, c:c + 1], SGg[:, c, :], ALU.mult, ALU.mult,
                  [pyk, 'bsc', sgk], ['yb'])

    def a2_o(Ts):
        for t, T in enumerate(Ts):
            out_proj_resid(YB, 'yb', t, WO,
                           lambda nb, T=T: X1[:, T - 1, nb * 512:(nb + 1) * 512], f'x1_{T}',
                           lambda nb, T=T: X1[:, T - 1, nb * 512:(nb + 1) * 512], f'x1_{T}')

    def a2_hook0():
        if not SKIPA:
            a2_fa([0])
            a2_fb([0], 1)
            a2_fa(groups[0])
        S.dma('pool', WG[:], d_wg[:, :, :, :], writes=['wg'], sem='c_wg')
        S.dma('pool', RP[:], d_rp[:, :, :, :], writes=['rp'], sem='c_rp')

    load_weights(WA, w_in_0, 3072, [(0, 1024), (1024, 1536), (1536, 2048)], WO, w_out_0, 8, hook=a2_hook0)
    if not SKIPA:
        a2_x([0], 1, False)
        a2_fb(groups[0], 0)
        a2_fa(groups[1])
        a2_x(groups[0], 0, True)
        a2_fb(groups[1], 1)
        for G in range(1, 6):
            nxt, ns = (groups[G + 1], (G + 1) % 2) if G + 1 < 6 else ([19], 0)
            a2_fs(nxt)
            a2_hook[0] = nxt
            a2_x(groups[G], G % 2, True)
            a2_p(groups[G - 1], (G - 1) % 2)
            a2_fb(nxt, ns)
            a2_o(groups[G - 1])
        a2_x([19], 0, False)
        a2_p(groups[5], 1)
        a2_o(groups[5])

    if stop_after == 'A2':
        dump_x1_and_finish()
        return finish(nc, S, A)

    S.barrier()
    A.reset(phase_mark)
    HB1 = A.alloc("HB1", [128, D], BF16)
    WA = A.alloc("WA", [128, 8, 2560], BF16)
    WO = A.alloc("WO", [128, 8, 1024], BF16)
    G1 = A.alloc("G1", [128, D], F32)
    MK = A.alloc("MK", [128, 3, 2, 128], BF16)
    COS = A.alloc("COS", [128, 18, 8], F32)
    SIN = A.alloc("SIN", [128, 18, 8], F32)
    ESK = A.alloc("ESK", [128, 16], F32)
    HT1 = [A.alloc("HT1", [128, 8, 128], BF16) for _ in range(2)]
    QKF = A.alloc("QKF", [128, 20, 64], F32)
    QKB = A.alloc("QKB", [128, 20, 64], BF16)
    RT = [A.alloc("RT", [128, 20, 8], F32) for _ in range(2)]
    KT = A.alloc("KT", [128, 2, 18 * 128], BF16)
    VV = A.alloc("VV", [128, 18, 4, 66], BF16)
    QT = [A.alloc("QT", [128, 8, 128], BF16) for _ in range(3)]
    SG = [A.alloc("SG", [128, D], BF16) for _ in range(3)]
    TGB = A.alloc("TGB", [128, 512], F32)
    PTG = [A.alloc("PTG", [128, 3, 4, 128], BF16) for _ in range(2)]
    ORAW = A.alloc("ORAW", [128, 16, 65], F32)
    DEN = A.alloc("DEN", [128, 16], F32)
    YB1 = A.alloc("YB1", [128, D], BF16)
    YT = A.alloc("YT", [128, 8, 128], BF16)
    OB = [A.alloc("OB", [128, D], F32)]

    S.dma('sp', G0[:], d_g1[:, :], writes=['g0'], sem='c_g0')
    S.dma('sp', G1[:], d_gf[:, :], writes=['g1'], sem='c_g1')
    S.dma('pool', MK[:], d_mk[:, :, :, :], writes=['mk'], sem='c_mk')
    S.dma('sp', COS[:], d_cos[:, :, :], writes=['cos'], sem='c_cos')
    S.dma('sp', SIN[:], d_sin[:, :, :], writes=['sin'], sem='c_sin')
    S.dma('sp', ESK[:], d_sink[:, :], writes=['esk'], sem='c_esk')
    S.act(ESK[:], ESK[:], AF.Exp, ['esk'], ['esk'])
    S.op('pool', lambda e: e.memset(VV[:, :, :, 64:65], 1.0), writes=['vones'])
    ring = {'ptb': 0, 'sc': 0, 'po': 0, 'pr': 0}

    def pool_bank(kind):
        base, cnt = {'sc': (0, 3), 'po': (3, 1), 'pr': (4, 2)}[kind]
        i = base + ring[kind] % cnt
        ring[kind] += 1
        return PB[i], f'pb{i}'

    b_st = {}

    def b1a_s(T):
        b_st[T] = norm_stats(X1[:, T - 1, :], f'x1_{T}')

    def b1a_h(T):
        st, stk = b_st[T]
        S.stt(HB1[:], X1[:, T - 1, :], st, G0[:], ALU.mult, ALU.mult, [f'x1_{T}', stk, 'g0'], ['hb1'])

    def b1a_n(T):
        b1a_s(T)
        b1a_h(T)

    def b1a_t(T):
        HTt, htk = HT1[T % 2], f'ht1{T % 2}'
        pt, ptk = next_pt()
        for c in range(8):
            S.tr(pt[:, c, :], HB1[:, c * 128:(c + 1) * 128], IDENT[:], ['hb1'], [ptk], signal=(c == 7))
        S.copy('dve', HTt[:, :, :], pt[:, :, :], [ptk], [htk])

    def b_proj_gen(T, c0, box):
        HTt, htk = HT1[T % 2], f'ht1{T % 2}'
        p, pk = pool_bank('pr')
        box.append((p, pk))
        for kc in range(8):
            S.mm(p[:, :], HTt[:, kc, :], WA[:, kc, c0:c0 + 512], kc == 0, kc == 7, [htk, f'WA{kc}_{c0 // 512}'], [pk])
            if kc == 3:
                yield

    def b_proj(T, c0):
        box = []
        for _ in b_proj_gen(T, c0, box):
            pass
        return box[0]

    def b1b_kv_gen(T):
        box = []
        yield from b_proj_gen(T, 1024, box)
        pkv, pkvk = box[0]
        S.copy('act', VV[:, T - 1, :, 0:64], pkv[:, 256:512].rearrange("p (g d) -> p g d", g=4),
               [pkvk], [f'vv{T}'])
        S.copy('dve', QKF[:, 16:20, :], pkv[:, 0:256].rearrange("p (g d) -> p g d", g=4), [pkvk], ['qkf'])

    def b1b_q_gen(T, qb):
        box = []
        yield from b_proj_gen(T, qb * 512, box)
        pq, pqk = box[0]
        for j in range(2):
            S.copy('act' if j else 'dve', QKF[:, qb * 8 + j:qb * 8 + 8:2, :],
                   pq[:, j * 256:(j + 1) * 256].rearrange("p (i d) -> p i d", i=4), [pqk], ['qkf'])

    def split(gen):
        def fa():
            next(gen)

        def fb():
            for _ in gen:
                pass
        return fa, fb

    def run(gen):
        for _ in gen:
            pass

    def b1b_kv(T):
        run(b1b_kv_gen(T))

    def b1b_q(T, qb):
        run(b1b_q_gen(T, qb))

    def b1b_g(T, gb):
        sg, sgk = SG[T % 3], f'sg{T % 3}'
        pg, pgk = b_proj(T, 1536 + gb * 512)
        S.act(TGB[:], pg[:, :], AF.Tanh, [pgk], ['tgb'], scale=0.5)
        S.stt(sg[:, gb * 512:(gb + 1) * 512], TGB[:], 1.0, pg[:, :], ALU.add, ALU.mult, ['tgb', pgk], [sgk])

    def b1b(T):
        b1b_kv(T)
        if 2 <= T <= 17:
            b1b_q(T, 0)
            b1b_q(T, 1)
            b1b_g(T, 0)
            b1b_g(T, 1)

    def b1c_rope(T):
        own = 2 <= T <= 17
        h0, nh = (0, 20) if own else (16, 4)
        hs = slice(h0, h0 + nh)
        cosb = COS[:, T - 1:T, :].broadcast_to([128, nh, 8])
        sinb = SIN[:, T - 1:T, :].broadcast_to([128, nh, 8])
        xa, xb = QKF[:, hs, 0:8], QKF[:, hs, 8:16]
        r = [RT[i][:, hs, :] for i in range(2)]
        S.tt('dve', r[0], xa, cosb, ALU.mult, ['qkf', 'cos'], ['rt0'])
        S.tt('dve', r[1], xb, sinb, ALU.mult, ['qkf', 'sin'], ['rt1'])
        S.tt('dve', QKB[:, hs, 0:8], r[0], r[1], ALU.subtract, ['rt0', 'rt1'], ['qkb_rot'])
        S.tt('dve', r[0], xb, cosb, ALU.mult, ['qkf', 'cos'], ['rt0'])
        S.tt('dve', r[1], xa, sinb, ALU.mult, ['qkf', 'sin'], ['rt1'])
        S.tt('dve', QKB[:, hs, 8:16], r[0], r[1], ALU.add, ['rt0', 'rt1'], ['qkb_rot'])
        S.copy('dve', QKB[:, hs, 16:64], QKF[:, hs, 16:64], ['qkf'], ['qkb'])

    def b1c_tr(T):
        own = 2 <= T <= 17
        pt, ptk = next_pt()
        for ch in range(2):
            S.tr(pt[:, ch, :], QKB[:, 16 + 2 * ch:18 + 2 * ch, :].rearrange("p h d -> p (h d)"), IDENT[:],
                 ['qkb', 'qkb_rot'], [ptk], signal=(ch == 1))
        S.copy('act', KT[:, :, (T - 1) * 128:T * 128], pt[:, 0:2, :], [ptk], [f'kt{T}'])
        if own:
            pt, ptk = next_pt()
            for c8 in range(8):
                S.tr(pt[:, c8, :], QKB[:, 2 * c8:2 * c8 + 2, :].rearrange("p h d -> p (h d)"), IDENT[:],
                     ['qkb', 'qkb_rot'], [ptk], signal=(c8 == 7))
            S.copy('dve', QT[T % 3][:, :, :], pt[:, :, :], [ptk], [f'qt{T % 3}'])

    def b2s(Tq, g, f1=None, f2=None):
        qt, qtk = QT[Tq % 3], f'qt{Tq % 3}'
        mk = MK[:, 1] if Tq == 2 else (MK[:, 2] if Tq == 17 else MK[:, 0])
        ch, pb = g // 2, (g % 2) * 64
        c0 = 4 if g >= 2 else 0
        pi = ring['ptb'] % 2
        ring['ptb'] += 1
        ptg, ptgk = PTG[pi], f'ptg{pi}'
        for j in (1, 0, 2):
            if j == 0 and f1 is not None:
                f1()
            if j == 2 and f2 is not None:
                f2()
            Tk = Tq - 1 + j
            ps, psk = pool_bank('sc')
            S.mm(ps[:, 0:512], KT[pb:pb + 64, ch, (Tk - 1) * 128:Tk * 128],
                 qt[pb:pb + 64, c0:c0 + 4, :].rearrange("p c q -> p (c q)"), True, True,
                 [f'kt{Tk}', qtk], [psk])
            S.act(ptg[:, j, :, :].rearrange("p h q -> p (h q)"), ps[:, 0:512], AF.Exp, [psk], [ptgk],
                  scale=0.125)
        S.tt('dve', ptg[:, 0:3:2, :, :], ptg[:, 0:3:2, :, :], mk.unsqueeze(2).broadcast_to([128, 2, 4, 128]),
             ALU.mult, [ptgk, 'mk'], [ptgk])
        return ptg, ptgk

    def b2v(Tq, g, ptg, ptgk):
        po, pok = pool_bank('po')
        for hh in range(4):
            for j in range(3):
                Tk = Tq - 1 + j
                S.mm(po[:, hh * 65:(hh + 1) * 65], ptg[:, j, hh, :], VV[:, Tk - 1, g, 0:65], j == 0, j == 2,
                     [ptgk, f'vv{Tk}', 'vones'], [pok], signal=(j == 2))
        S.copy('act', ORAW[:, g * 4:(g + 1) * 4, :], po[:, 0:260].rearrange("p (h d) -> p h d", h=4),
               [pok], ['oraw'])

    def b2f_a(Tq):
        sg, sgk = SG[Tq % 3], f'sg{Tq % 3}'
        S.tt('dve', DEN[:], ORAW[:, :, 64], ESK[:], ALU.add, ['oraw', 'esk'], ['den'])
        S.op('dve', lambda e: e.reciprocal(out=DEN[:], in_=DEN[:]), ['den'], ['den'])
        S.tt('dve', ORAW[:, :, 0:64], ORAW[:, :, 0:64], DEN[:].unsqueeze(2).broadcast_to([128, 16, 64]),
             ALU.mult, ['oraw', 'den'], ['oraw'])
        S.stt(YB1[:].rearrange("p (h d) -> p h d", h=16), ORAW[:, :, 0:64], 0.5,
              sg[:].rearrange("p (h d) -> p h d", h=16), ALU.mult, ALU.mult, ['oraw', sgk], ['yb1'])

    def b2f_b1(Tq):
        pt, ptk = next_pt()
        for c in range(8):
            S.tr(pt[:, c, :], YB1[:, c * 128:(c + 1) * 128], IDENT[:], ['yb1'], [ptk], signal=(c == 7))
        S.copy('act', YT[:, :, :], pt[:, :, :], [ptk], ['yt'])

    def b2f_o(Tq, nb):
        run(b2f_o_gen(Tq, nb))

    def b2f_o_gen(Tq, nb):
        ob, obk = OB[0], 'ob0'
        pf, pfk = pool_bank('pr')
        for c in range(8):
            S.mm(pf[:, :], YT[:, c, :], WO[:, c, nb * 512:(nb + 1) * 512], c == 0, c == 7,
                 ['yt', f'WO{c}'], [pfk])
            if c == 3:
                yield
        S.tt('dve', ob[:, nb * 512:(nb + 1) * 512], pf[:, :], X1[:, Tq - 1, nb * 512:(nb + 1) * 512],
             ALU.add, [pfk, f'x1_{Tq}'], [obk])
        if nb == 1:
            b_st['f'] = norm_stats(ob[:], obk)

    def b2f_n(Tq):
        ob, obk = OB[0], 'ob0'
        st, stk = b_st['f']
        S.stt(ob[:], ob[:], st, G1[:], ALU.mult, ALU.mult, [obk, stk, 'g1'], [obk])
        S.dma('sp', out[(Tq - 2) * 128:(Tq - 1) * 128, :], ob[:], reads=[obk], writes=[f'out{Tq}'],
              sem='out0')

    load_weights(WA, w_in_1, 0, [(1024, 1536), (0, 512), (512, 1024), (1536, 2048), (2048, 2560)], WO, w_out_1, 0,
                 hook=lambda: (b1a_s(1), b1a_s(2)))
    b1a_h(1)
    b1a_t(1)
    b1b_kv(1)
    b1a_s(3)
    b1a_h(2)
    b1c_rope(1)
    b1a_t(2)
    b1b_kv(2)
    b1c_tr(1)
    b1b_q(2, 0)
    b1b_q(2, 1)
    b1a_h(3)
    b1c_rope(2)
    b1b_g(2, 0)
    b1a_t(3)
    b1b_g(2, 1)
    b1b_kv(3)
    b1c_tr(2)
    b1a_s(4)
    b1b_q(3, 0)
    b1b_q(3, 1)
    b1c_rope(3)
    b1b_g(3, 0)
    b1b_g(3, 1)
    b1a_h(4)
    b1c_tr(3)
    for Tq in range(2, 18):
        T2 = Tq + 2
        nxt = T2 <= 18
        own2 = nxt and T2 <= 17
        prev = Tq > 2
        nop = (None, None)
        kvA, kvB = split(b1b_kv_gen(T2)) if nxt else nop
        q0A, q0B = split(b1b_q_gen(T2, 0)) if own2 else nop
        q1A, q1B = split(b1b_q_gen(T2, 1)) if own2 else nop
        o0A, o0B = split(b2f_o_gen(Tq - 1, 0)) if prev else nop
        o1A, o1B = split(b2f_o_gen(Tq - 1, 1)) if prev else nop
        p0 = b2s(Tq, 0, (lambda: b1a_t(T2)) if nxt else None, kvA)
        p1 = b2s(Tq, 1, kvB, q0A)
        b2v(Tq, 0, *p0)
        if T2 + 1 <= 18:
            b1a_s(T2 + 1)
        if prev:
            b2f_b1(Tq - 1)
        p2 = b2s(Tq, 2, q0B, o0A)
        b2v(Tq, 1, *p1)
        p3 = b2s(Tq, 3, o0B, q1A)
        b2v(Tq, 2, *p2)
        if T2 + 1 <= 18:
            b1a_h(T2 + 1)
        if q1B is not None:
            q1B()
        if nxt:
            b1c_rope(T2)
        if prev:
            o1A()
            o1B()
        b2v(Tq, 3, *p3)
        if own2:
            b1b_g(T2, 0)
        if prev:
            b2f_n(Tq - 1)
        if own2:
            b1b_g(T2, 1)
        b2f_a(Tq)
        if nxt:
            b1c_tr(T2)
    b2f_b1(17)
    b2f_o(17, 0)
    b2f_o(17, 1)
    b2f_n(17)
    S.wait_all('sp', [f'out{Tq}' for Tq in range(2, 18)])
    return finish(nc, S, A)


def finish(nc, S, A):
    with nc.Block() as block:
        S.emit(block)
    return nc


def _pool_mats(first_edge, last_edge):
    R = np.zeros((128, 5, 4, 128), np.float32)
    tt = np.arange(128)
    for g, w in enumerate((2, 4, 8, 16)):
        hw = w // 2
        for blk in range(5):
            off = {0: -128, 1: 0, 2: 128, 3: 0, 4: 0}[blk]
            tp = tt[:, None] + off
            t = tt[None, :]
            inwin = (tp >= t - hw) & (tp < t + hw)
            cnt = np.full((1, 128), float(w), np.float32)
            if blk == 3 and first_edge:
                cnt = (np.minimum(t + hw, 10 ** 9) - np.maximum(t - hw, 0)).astype(np.float32)
            if blk == 4 and last_edge:
                cnt = (np.minimum(t + hw, 128) - (t - hw)).astype(np.float32)
            m = inwin.astype(np.float32) / cnt
            if blk in (1, 3, 4):
                m = m - np.eye(128, dtype=np.float32)
            R[:, blk, g, :] = m
    return R


def _prep(inputs):
    f = lambda a: np.ascontiguousarray(np.asarray(a, dtype=np.float32))
    x = f(inputs["x"])
    common = {
        "w_in_0": f(inputs["w_in_0"]), "w_out_0": f(inputs["w_out_0"]),
        "w_in_1": f(inputs["w_in_1"]), "w_out_1": f(inputs["w_out_1"]),
        "g0_bc": f(np.broadcast_to(f(inputs["norm_0"])[None, :], (128, D))),
        "g1_bc": f(np.broadcast_to(f(inputs["norm_1"])[None, :], (128, D))),
        "gf_bc": f(np.broadcast_to(f(inputs["final_norm"])[None, :], (128, D))),
        "ident": np.eye(128, dtype=np.float32),
        "wsT": f(f(inputs["a_spatial_w_0"]).transpose(2, 0, 1)),
        "b_bc": f(np.broadcast_to(f(inputs["a_spatial_b_0"])[None, :, :], (128, 4, 128))),
        "gv_col": f(f(inputs["a_v_norm_0"]).reshape(8, 128).T),
        "bsc_col": f(f(inputs["b_scale_0"]).reshape(8, 128).T),
        "wg": f(f(inputs["b_group_w_0"]).reshape(4, 2, 128, 256).transpose(2, 0, 1, 3)),
        "sink_bc": f(np.broadcast_to(f(inputs["sink_1"])[None, :], (128, 16))),
    }
    kk = np.arange(128)[:, None]
    qq = np.arange(128)[None, :]
    maskP = (kk >= qq).astype(np.float32)
    maskN = (kk <= qq).astype(np.float32)
    inv = (500000.0 ** (-np.arange(0, 16, 2, dtype=np.float32) / 16.0)).astype(np.float32)
    maps = []
    for c in range(N_CORES):
        b, q = c // 4, c % 4
        start = q * 2048 - 256
        xpad = np.zeros((NT * 128, D), np.float32)
        lo, hi = max(start, 0), min(start + NT * 128, S_LEN)
        xpad[lo - start:hi - start] = x[b, lo:hi]
        masks = np.zeros((128, 3, 2, 128), np.float32)
        masks[:, 0, 0], masks[:, 0, 1] = maskP, maskN
        masks[:, 1, 0], masks[:, 1, 1] = (0.0 if q == 0 else maskP), maskN
        masks[:, 2, 0], masks[:, 2, 1] = maskP, (0.0 if q == 3 else maskN)
        pos = (start + 128 + np.arange(18 * 128)).astype(np.float32)
        ang = pos[:, None] * inv[None, :]
        cos = np.cos(ang).astype(np.float32).reshape(18, 128, 8).transpose(1, 0, 2)
        sin = np.sin(ang).astype(np.float32).reshape(18, 128, 8).transpose(1, 0, 2)
        m = dict(common)
        m.update({"xp": xpad, "masks": masks, "cos": f(cos), "sin": f(sin),
                  "rpool": _pool_mats(q == 0, q == 3)})
        maps.append(m)
    return maps


_NC_CACHE = {}


def kernel(**inputs):
    maps = _prep(inputs)
    if "nc" not in _NC_CACHE:
        _NC_CACHE["nc"] = build()
    nc = _NC_CACHE["nc"]
    res = run_bass_kernel_spmd(nc, maps, core_ids=list(range(N_CORES)))
    outp = np.empty((2, S_LEN, D), np.float32)
    for c in range(N_CORES):
        b, q = c // 4, c % 4
        outp[b, q * 2048:(q + 1) * 2048] = res.results[c]["out"]
    return outp
```

```python
import numpy as np
import concourse.bass as bass
import concourse.mybir as mybir
from concourse.bass_utils import run_bass_kernel_spmd
from concourse.alu_op_type import AluOpType as ALU

AF = mybir.ActivationFunctionType
F32 = mybir.dt.float32
BF16 = mybir.dt.bfloat16

SBUF_LO = 16512
SBUF_HI = 229344

D = 1024
S_LEN = 8192
NT = 20
EPS = 1e-6
N_CORES = 8
QPAIR_HA = [0, 1, 2, 3, 8, 9, 10, 11]


class Sched:
    def __init__(self, nc):
        self.nc = nc
        self.names = ['pe', 'act', 'dve', 'pool', 'sp']
        self.prog = {e: [] for e in self.names}
        self.cnt = {e: 0 for e in self.names}
        self.sems = {}
        self.known = {e: {} for e in self.names}
        self.lastw = {}
        self.readers = {}
        self.dcount = {}
        self.dkeys = {}
        self.nwaits = 0

    def sem(self, sk):
        if sk not in self.sems:
            self.sems[sk] = self.nc.alloc_semaphore(name=sk.replace(':', '_'))
        return self.sems[sk]

    def _waits(self, eng, reads, writes):
        deps = {}

        def add(sk, v, raw):
            if sk == 'E:' + eng and eng == 'pe':
                return
            if deps.get(sk, 0) < v:
                deps[sk] = v

        for k in reads:
            if k in self.lastw:
                add(*self.lastw[k], True)
            if k[:2] in ('pb', 'pt'):
                for sk, v in self.readers.get(k, {}).items():
                    add(sk, v, False)
        for k in writes:
            if k in self.lastw:
                add(*self.lastw[k], False)
            for sk, v in self.readers.get(k, {}).items():
                add(sk, v, False)
        waits = []
        for sk, v in deps.items():
            if self.known[eng].get(sk, 0) >= v:
                continue
            self.known[eng][sk] = v
            waits.append((sk, v))
            self.sem(sk)
        self.nwaits += len(waits)
        return waits

    def _record(self, me, reads, writes):
        for k in reads:
            self.readers.setdefault(k, {})[me[0]] = me[1]
        for k in writes:
            self.lastw[k] = me
            self.readers[k] = {}

    def op(self, eng, fn, reads=(), writes=(), signal=True):
        waits = self._waits(eng, reads, writes)
        sk = 'E:' + eng
        self.sem(sk)
        if signal:
            self.cnt[eng] += 1
            me = (sk, self.cnt[eng])
        else:
            me = (sk, self.cnt[eng] + 1)
        self.prog[eng].append((waits, fn, me if signal else None))
        self._record(me, reads, writes)

    def dma(self, queue, out, in_, reads=(), writes=(), sem=None):
        waits = self._waits(queue, reads, writes)
        sk = 'D:' + sem
        self.sem(sk)
        self.dcount[sk] = self.dcount.get(sk, 0) + 16
        me = (sk, self.dcount[sk])
        self.prog[queue].append((waits, lambda e: e.dma_start(out=out, in_=in_), me))
        self._record(me, reads, writes)
        ks = self.dkeys.setdefault(sk, set())
        ks.update(writes)
        for k in ks:
            if k in self.lastw and self.lastw[k][0] == sk:
                self.lastw[k] = me

    def wait_all(self, eng, keys):
        waits = self._waits(eng, keys, ())
        self.prog[eng].append((waits, None, None))

    def barrier(self):
        tot = {}
        for e in self.names:
            if self.cnt[e]:
                tot['E:' + e] = self.cnt[e]
        for sk, v in self.dcount.items():
            tot[sk] = v
        for e in self.names:
            waits = []
            for sk, v in tot.items():
                if sk == 'E:' + e:
                    continue
                if self.known[e].get(sk, 0) >= v:
                    continue
                self.known[e][sk] = v
                waits.append((sk, v))
            self.prog[e].append((waits, None, None))
        self.lastw = {}
        self.readers = {}

    def emit(self, block):
        deco = {'pe': block.tensor, 'act': block.scalar, 'dve': block.vector,
                'pool': block.gpsimd, 'sp': block.sync}
        for e in self.names:
            prog = self.prog[e]

            def body(eng, prog=prog):
                for waits, fn, me in prog:
                    for sk, v in waits:
                        eng.wait_ge(self.sems[sk], v)
                    if fn is not None:
                        inst = fn(eng)
                        if me is not None:
                            inst.then_inc(self.sems[me[0]], 16 if me[0][0] == 'D' else 1)

            deco[e](body)

    def mm(self, out, lhsT, rhs, start, stop, reads, writes, signal=None):
        if signal is None:
            signal = stop
        self.op('pe', lambda e: e.matmul(out, lhsT, rhs, start=start, stop=stop), reads, writes, signal)

    def tr(self, out, in_, ident, reads, writes, signal=True):
        self.op('pe', lambda e: e.transpose(out, in_, ident), list(reads) + ['ident'], writes, signal)

    def act(self, out, in_, func, reads, writes, **kw):
        self.op('act', lambda e: e.activation(out=out, in_=in_, func=func, **kw), reads, writes)

    def copy(self, eng, out, in_, reads, writes):
        if eng == 'act':
            self.op('act', lambda e: e.activation(out=out, in_=in_, func=AF.Copy), reads, writes)
        else:
            self.op(eng, lambda e: e.tensor_copy(out=out, in_=in_), reads, writes)

    def tt(self, eng, out, in0, in1, op, reads, writes):
        self.op(eng, lambda e: e.tensor_tensor(out=out, in0=in0, in1=in1, op=op), reads, writes)

    def ts(self, eng, out, in0, s1, s2, op0, op1, reads, writes):
        if op1 is None:
            self.op(eng, lambda e: e.tensor_scalar(out=out, in0=in0, scalar1=s1, scalar2=None, op0=op0),
                    reads, writes)
        else:
            self.op(eng, lambda e: e.tensor_scalar(out=out, in0=in0, scalar1=s1, scalar2=s2, op0=op0, op1=op1),
                    reads, writes)

    def stt(self, out, in0, scalar, in1, op0, op1, reads, writes):
        self.op('dve', lambda e: e.scalar_tensor_tensor(out=out, in0=in0, scalar=scalar, in1=in1,
                                                        op0=op0, op1=op1), reads, writes)


class Arena:
    def __init__(self, nc):
        self.nc = nc
        self.off = SBUF_LO
        self.n = 0
        self.peak = 0

    def alloc(self, name, shape, dtype):
        esz = 4 if dtype == F32 else 2
        nbytes = int(np.prod(shape[1:])) * esz
        nbytes = (nbytes + 63) // 64 * 64
        assert self.off + nbytes <= SBUF_HI, f"SBUF overflow allocating {name}: {self.off + nbytes - SBUF_HI}"
        t = self.nc.alloc_sbuf_tensor_at(f"{name}_{self.n}", list(shape), dtype, offset=self.off)
        self.n += 1
        self.off += nbytes
        self.peak = max(self.peak, self.off)
        return t

    def mark(self):
        return self.off

    def reset(self, m):
        self.off = m


def build(stop_after=None):
    nc = bass.Bass("TRN2", target_bir_lowering=False)
    S = Sched(nc)
    A = Arena(nc)

    def din(name, shape):
        return nc.dram_tensor(name, list(shape), F32, kind="ExternalInput").ap()

    xp = din("xp", [NT * 128, D])
    w_in_0 = din("w_in_0", [1024, 5120]).rearrange("(kc p) n -> p kc n", p=128)
    w_out_0 = din("w_out_0", [2048, 1024]).rearrange("(kc p) n -> p kc n", p=128)
    w_in_1 = din("w_in_1", [1024, 2560]).rearrange("(kc p) n -> p kc n", p=128)
    w_out_1 = din("w_out_1", [1024, 1024]).rearrange("(kc p) n -> p kc n", p=128)
    d_g0 = din("g0_bc", [128, D])
    d_g1 = din("g1_bc", [128, D])
    d_gf = din("gf_bc", [128, D])
    d_ident = din("ident", [128, 128])
    d_wsT = din("wsT", [128, 4, 128])
    d_bbc = din("b_bc", [128, 4, 128])
    d_gv = din("gv_col", [128, 8])
    d_bsc = din("bsc_col", [128, 8])
    d_wg = din("wg", [128, 4, 2, 256])
    d_rp = din("rpool", [128, 5, 4, 128])
    d_mk = din("masks", [128, 3, 2, 128])
    d_cos = din("cos", [128, 18, 8])
    d_sin = din("sin", [128, 18, 8])
    d_sink = din("sink_bc", [128, 16])
    out = nc.dram_tensor("out", [16 * 128, D], F32, kind="ExternalOutput").ap()
    dbg = None
    if stop_after:
        dbg = nc.dram_tensor("dbg", [18 * 128, D], F32, kind="ExternalOutput").ap()

    X1 = A.alloc("X1", [128, 18, D], F32)
    G0 = A.alloc("G0", [128, D], F32)
    HB = [None, None, None]
    IDENT = A.alloc("IDENT", [128, 128], BF16)
    MHALF = A.alloc("MHALF", [128, 1], F32)
    ST = A.alloc("ST", [128, 128], F32)
    JUNK = A.alloc("JUNK", [128, D], BF16)

    PT = [nc.alloc_psum_tensor(f"pt{i}", [128, 8, 128], BF16) for i in range(2)]
    PB = [nc.alloc_psum_tensor(f"pb{i}", [128, 512], F32) for i in range(6)]
    rr = {'pt': 0, 'pb': 0, 'st': 0, 'hb': 0}

    def next_pt():
        i = rr['pt'] % 2
        rr['pt'] += 1
        return PT[i], f'pt{i}'

    def next_pb():
        i = rr['pb'] % 6
        rr['pb'] += 1
        return PB[i], f'pb{i}'

    def new_stat():
        i = rr['st']
        rr['st'] += 1
        assert i < 128
        return ST[:, i:i + 1], f'st{i}'

    S.dma('pool', IDENT[:], d_ident[:, :], writes=['ident'], sem='c_ident')
    S.op('pool', lambda e: e.memset(MHALF[:], -0.5), writes=['mhalf'])

    def rstd_from_sumsq(st, stk):
        S.ts('pool', st, st, 1.0 / D, EPS, ALU.mult, ALU.add, [stk], [stk])
        S.tt('pool', st, st, MHALF[:], ALU.pow, [stk, 'mhalf'], [stk])

    def sumsq(x_ap, xkey):
        st, stk = new_stat()
        S.act(JUNK[:], x_ap, AF.Square, [xkey], ['junk', stk], accum_out=st)
        return st, stk

    def norm_stats(x_ap, xkey):
        st, stk = sumsq(x_ap, xkey)
        rstd_from_sumsq(st, stk)
        return st, stk

    def norm_apply(x_ap, xkey, gain, gkey, i, st, stk):
        S.stt(HB[i][:], x_ap, st, gain, ALU.mult, ALU.mult, [xkey, stk, gkey], [f'hb{i}'])

    def norm_h(x_ap, xkey, gain, gkey, i):
        st, stk = norm_stats(x_ap, xkey)
        norm_apply(x_ap, xkey, gain, gkey, i, st, stk)

    def h_to_hT(i, dst_ap, dst_key, evac):
        hb, hkey = HB[i], f'hb{i}'
        pt, ptk = next_pt()
        for c in range(8):
            S.tr(pt[:, c, :], hb[:, c * 128:(c + 1) * 128], IDENT[:], [hkey], [ptk], signal=(c == 7))
        S.copy(evac, dst_ap, pt[:, :, :], [ptk], [dst_key])

    def load_weights(WA, w_in, c0, parts, WO, w_out, k0, hook=None):
        a, b = parts[0]
        for kc in range(8):
            S.dma('pool', WA[:, kc, a:b], w_in[:, kc, c0 + a:c0 + b],
                  writes=[f'WA{kc}_{blk}' for blk in range(a // 512, b // 512)], sem=f'WA{kc}p0')
        if hook is not None:
            hook()
        for pi, (a, b) in enumerate(parts[1:]):
            S.dma('pool', WA[:, :, a:b], w_in[:, :, c0 + a:c0 + b],
                  writes=[f'WA{kc}_{blk}' for kc in range(8) for blk in range(a // 512, b // 512)],
                  sem=f'WAp{pi + 1}')
        S.dma('pool', WO[:, :, :], w_out[:, k0:k0 + 8, :], writes=[f'WO{kc}' for kc in range(8)], sem='WOall')

    def out_proj_resid(Y, ykey, t, WO, dst_ap_fn, dst_key, res_ap_fn, res_key):
        for nb in range(2):
            po, pok = next_pb()
            for c in range(8):
                S.mm(po[:, :], Y[:, c, t * 128:(t + 1) * 128], WO[:, c, nb * 512:(nb + 1) * 512],
                     c == 0, c == 7, [ykey, f'WO{c}'], [pok])
            S.tt('dve', dst_ap_fn(nb), po[:, :], res_ap_fn(nb), ALU.add, [pok, res_key], [dst_key])

    phase_mark = A.mark()
    groups = [[1 + 3 * G + t for t in range(3)] for G in range(6)]

    for _i in range(3):
        HB[_i] = A.alloc("HB", [128, D], BF16)
    WA = A.alloc("WA", [128, 8, 3072], BF16)
    WO = A.alloc("WO", [128, 8, 1024], BF16)
    WST = A.alloc("WST", [128, 4, 128], BF16)
    BBC = A.alloc("BBC", [128, 4, 128], F32)
    GV = A.alloc("GV", [128, 8], F32)
    HT = [A.alloc("HT", [128, 8, 384], BF16) for _ in range(2)]
    VB = [A.alloc("VB", [128, D], BF16) for _ in range(3)]
    WS = [A.alloc("WS", [128, 4, 128], BF16) for _ in range(3)]
    UC = [A.alloc("UC", [128, 384], F32) for _ in range(2)]
    TG = [A.alloc("TG", [128, 384], F32) for _ in range(2)]
    M1 = [A.alloc("M1", [128, 384], F32) for _ in range(2)]
    YA = [A.alloc("YA", [128, 8, 384], BF16) for _ in range(2)]

    for T in groups[0]:
        S.dma('sp', X1[:, T - 1, :], xp[T * 128:(T + 1) * 128, :], writes=[f'x1_{T}'], sem='x1g0')
    S.dma('sp', G0[:], d_g0[:, :], writes=['g0'], sem='c_g0')
    S.dma('pool', WST[:], d_wsT[:, :, :], writes=['wst'], sem='c_wst')
    S.dma('sp', BBC[:], d_bbc[:, :, :], writes=['bbc'], sem='c_bbc')
    S.dma('sp', GV[:], d_gv[:, :], writes=['gv'], sem='c_gv')

    SKIPA = False
    if SKIPA:
        for T in range(1, 19):
            S.dma('sp', X1[:, T - 1, :], xp[T * 128:(T + 1) * 128, :], writes=[f'x1_{T}'], sem='x1all')
    a1_st = {}

    def a1_front_s(G):
        for t, T in enumerate(groups[G]):
            if G > 0:
                S.dma('sp', X1[:, T - 1, :], xp[T * 128:(T + 1) * 128, :], writes=[f'x1_{T}'], sem=f'x1g{G}')
        for t, T in enumerate(groups[G]):
            a1_st[T] = norm_stats(X1[:, T - 1, :], f'x1_{T}')

    def a1_front_h(G, only=None):
        for t, T in enumerate(groups[G]):
            if only is None or only == t:
                norm_apply(X1[:, T - 1, :], f'x1_{T}', G0[:], 'g0', t, *a1_st[T])

    def a1_front_a(G):
        a1_front_s(G)
        a1_front_h(G)

    def a1_front_b(G):
        HTg, htk = HT[G % 2], f'ht{G % 2}'
        for t, T in enumerate(groups[G]):
            h_to_hT(t, HTg[:, :, t * 128:(t + 1) * 128], htk + f'_{t}', 'act')

    def a1_mid_v(G):
        Ts = groups[G]
        HTg, htk = HT[G % 2], f'ht{G % 2}'
        for t, T in enumerate(Ts):
            for nb in range(2):
                pv, pvk = next_pb()
                for kc in range(8):
                    S.mm(pv[:, :], HTg[:, kc, t * 128:(t + 1) * 128],
                         WA[:, kc, 1024 + nb * 512:1024 + (nb + 1) * 512], kc == 0, kc == 7,
                         [htk + f'_{t}', f'WA{kc}_{2 + nb}'], [pvk])
                S.act(VB[t][:, nb * 512:(nb + 1) * 512], pv[:, :], AF.Gelu_apprx_tanh, [pvk], [f'vb{t}'])
            st, stk = sumsq(VB[t][:], f'vb{t}')
            rstd_from_sumsq(st, stk)
            S.ts('dve', WS[t][:], WST[:], st, None, ALU.mult, None, ['wst', stk], [f'ws{t}'])

    def a1_mid_c(G):
        HTg, htk = HT[G % 2], f'ht{G % 2}'
        YAg, yak = YA[G % 2], f'ya{G % 2}'
        for c in range(8):
            if c == 1 and G + 1 < 6:
                a1_front_s(G + 1)
            if c in (3, 4, 5) and G + 1 < 6:
                a1_front_h(G + 1, c - 3)
            hd = c // 2
            uc, uck = UC[c % 2], f'uc{c % 2}'
            tg, tgk = TG[c % 2], f'tg{c % 2}'
            m1, m1k = M1[c % 2], f'm1{c % 2}'
            pu, puk = next_pb()
            for kc in range(8):
                S.mm(pu[:, 0:384], WA[:, kc, c * 128:(c + 1) * 128], HTg[:, kc, :], kc == 0, kc == 7,
                     [htk + '_0', htk + '_1', htk + '_2', f'WA{kc}_{c // 4}'], [puk])
            S.act(uc[:], pu[:, 0:384], AF.Gelu_apprx_tanh, [puk], [uck])
            pg, pgk = next_pb()
            for kc in range(8):
                S.mm(pg[:, 0:384], WA[:, kc, 2048 + c * 128:2048 + (c + 1) * 128], HTg[:, kc, :],
                     kc == 0, kc == 7, [htk + '_0', htk + '_1', htk + '_2', f'WA{kc}_{4 + c // 4}'], [pgk])
            S.act(tg[:], pg[:, 0:384], AF.Tanh, [pgk], [tgk], scale=0.5)
            S.stt(tg[:], tg[:], 1.0, pg[:, 0:384], ALU.add, ALU.mult, [tgk, pgk], [tgk])
            psp, pspk = next_pb()
            for t in range(3):
                S.mm(psp[:, t * 128:(t + 1) * 128], VB[t][:, c * 128:(c + 1) * 128], WS[t][:, hd, :],
                     True, True, [f'vb{t}', f'ws{t}'], [pspk], signal=(t == 2))
            S.stt(m1[:].rearrange("p (t i) -> p t i", t=3), psp[:, 0:384].rearrange("p (t i) -> p t i", t=3),
                  GV[:, c:c + 1], BBC[:, hd:hd + 1, :].broadcast_to([128, 3, 128]), ALU.mult, ALU.add,
                  [pspk, 'gv', 'bbc'], [m1k])
            S.tt('pool', m1[:], m1[:], uc[:], ALU.mult, [m1k, uck], [m1k])
            S.stt(YAg[:, c, :], m1[:], 0.5, tg[:], ALU.mult, ALU.mult, [m1k, tgk], [yak])

    def a1_back(G):
        YAg, yak = YA[G % 2], f'ya{G % 2}'
        for t, T in enumerate(groups[G]):
            out_proj_resid(YAg, yak, t, WO,
                           lambda nb, T=T: X1[:, T - 1, nb * 512:(nb + 1) * 512], f'x1_{T}',
                           lambda nb, T=T: X1[:, T - 1, nb * 512:(nb + 1) * 512], f'x1_{T}')

    load_weights(WA, w_in_0, 0, [(1024, 1536), (1536, 2048), (0, 512), (2048, 2560), (512, 1024), (2560, 3072)], WO, w_out_0, 0,
                 hook=(None if SKIPA else (lambda: a1_front_a(0))))
    if not SKIPA:
        a1_front_b(0)
        a1_mid_v(0)
        for G in range(6):
            a1_mid_c(G)
            if G + 1 < 6:
                a1_front_b(G + 1)
                a1_mid_v(G + 1)
            a1_back(G)

    def dump_x1_and_finish():
        S.barrier()
        for T in range(1, 19):
            S.dma('sp', dbg[(T - 1) * 128:T * 128, :], X1[:, T - 1, :], writes=[f'dbg{T}'], sem='dbg')
        S.wait_all('sp', [f'dbg{T}' for T in range(1, 19)])

    if stop_after == 'A1':
        dump_x1_and_finish()
        return finish(nc, S, A)

    S.barrier()
    A.reset(phase_mark)
    for _i in range(3):
        HB[_i] = A.alloc("HB", [128, D], BF16)
    WA = A.alloc("WA", [128, 8, 2048], BF16)
    WO = A.alloc("WO", [128, 8, 1024], BF16)
    WG = A.alloc("WG", [128, 4, 2, 256], BF16)
    RP = A.alloc("RP", [128, 5, 4, 128], BF16)
    BSC = A.alloc("BSC", [128, 8], F32)
    XIN = [A.alloc("XIN", [128, D], F32) for _ in range(3)]
    HT = [A.alloc("HT", [128, 8, 384], BF16) for _ in range(2)]
    BX = [A.alloc("BX", [128, D], BF16) for _ in range(7)]
    a2_hook = [None]
    SGB = [A.alloc("SGB", [128, 8, 384], BF16) for _ in range(2)]
    TGC = [A.alloc("TGC", [128, 384], F32) for _ in range(2)]
    PP = A.alloc("PP", [128, 8, 384], BF16)
    YB = A.alloc("YB", [128, 8, 384], BF16)

    S.dma('sp', BSC[:], d_bsc[:, :], writes=['bsc'], sem='c_bsc')
    S.ts('dve', BSC[:], BSC[:], 0.5, None, ALU.mult, None, ['bsc'], ['bsc'])
    xin_n = [0]

    a2_st = {}

    def a2_fs(Ts):
        for t, T in enumerate(Ts):
            xi = xin_n[0] % 3
            xin_n[0] += 1
            S.dma('sp', XIN[xi][:], xp[T * 128:(T + 1) * 128, :], writes=[f'xin{xi}'], sem=f'xin{xi}')
            a2_st[T] = (xi,) + norm_stats(XIN[xi][:], f'xin{xi}')

    def a2_fh(Ts, only=None):
        for t, T in enumerate(Ts):
            if only is None or only == t:
                xi, st, stk = a2_st[T]
                norm_apply(XIN[xi][:], f'xin{xi}', G0[:], 'g0', t, st, stk)

    def a2_fa(Ts):
        a2_fs(Ts)
        a2_fh(Ts)

    def a2_fb(Ts, slot):
        HTg, htk = HT[slot], f'ht{slot}'
        for t, T in enumerate(Ts):
            h_to_hT(t, HTg[:, :, t * 128:(t + 1) * 128], htk + f'_{t}', 'act')

    def a2_x(Ts, slot, full):
        HTg, htk = HT[slot], f'ht{slot}'
        for t, T in enumerate(Ts):
            for nb in range(2):
                pb, pbk = next_pb()
                for kc in range(8):
                    S.mm(pb[:, :], HTg[:, kc, t * 128:(t + 1) * 128], WA[:, kc, nb * 512:(nb + 1) * 512],
                         kc == 0, kc == 7, [htk + f'_{t}', f'WA{kc}_{nb}'], [pbk])
                S.copy('act' if nb else 'dve', BX[T % 7][:, nb * 512:(nb + 1) * 512], pb[:, :],
                       [pbk], [f'bx{T % 7}'])
        if not full:
            return
        SGg, sgk = SGB[slot], f'sgb{slot}'
        for c in range(8):
            if c in (3, 4, 5) and a2_hook[0] is not None:
                a2_fh(a2_hook[0], c - 3)
                if c == 5:
                    a2_hook[0] = None
            tgc, tgck = TGC[c % 2], f'tgc{c % 2}'
            pg, pgk = next_pb()
            for kc in range(8):
                S.mm(pg[:, 0:384], WA[:, kc, 1024 + c * 128:1024 + (c + 1) * 128], HTg[:, kc, :],
                     kc == 0, kc == 7, [htk + '_0', htk + '_1', htk + '_2', f'WA{kc}_{2 + c // 4}'], [pgk])
            S.act(tgc[:], pg[:, 0:384], AF.Tanh, [pgk], [tgck], scale=0.5)
            S.stt(SGg[:, c, :], tgc[:], 1.0, pg[:, 0:384], ALU.add, ALU.mult, [tgck, pgk], [sgk])

    def a2_p(Ts, slot):
        SGg, sgk = SGB[slot], f'sgb{slot}'

        def pool_chunk(c):
            g = c // 2
            pp, ppk = next_pb()
            for t, T in enumerate(Ts):
                cur = 3 if T == 2 else (4 if T == 17 else 1)
                cs = slice(c * 128, (c + 1) * 128)
                o = pp[:, t * 128:(t + 1) * 128]
                S.mm(o, BX[(T - 1) % 7][:, cs], RP[:, 0, g, :], True, False, [f'bx{(T - 1) % 7}', 'rp'], [ppk])
                S.mm(o, BX[T % 7][:, cs], RP[:, cur, g, :], False, False, [f'bx{T % 7}', 'rp'], [ppk])
                S.mm(o, BX[(T + 1) % 7][:, cs], RP[:, 2, g, :], False, True, [f'bx{(T + 1) % 7}', 'rp'], [ppk],
                     signal=(t == 2))
            S.copy('act', PP[:, c, :], pp[:, 0:384], [ppk], [f'pp{c}'])

        def grp(g):
            for ec in range(2):
                c = g * 2 + ec
                py, pyk = next_pb()
                for dc in range(2):
                    S.mm(py[:, 0:384], WG[:, g, dc, ec * 128:(ec + 1) * 128], PP[:, g * 2 + dc, :],
                         dc == 0, dc == 1, ['wg', f'pp{g * 2 + dc}'], [pyk])
                S.stt(YB[:, c, :], py[:, 0:384], BSC[:, c:c + 1], SGg[:, c, :], ALU.mult, ALU.mult,
                      [pyk, 'bsc', sgk], ['yb'])

        pool_chunk(0)
        pool_chunk(1)
        for g in range(4):
            if 2 * g + 2 < 8:
                pool_chunk(2 * g + 2)
            grp(g)
            if 2 * g + 3 < 8:
                pool_chunk(2 * g + 3)

    def a2_o(Ts):
        for t, T in enumerate(Ts):
            out_proj_resid(YB, 'yb', t, WO,
                           lambda nb, T=T: X1[:, T - 1, nb * 512:(nb + 1) * 512], f'x1_{T}',
                           lambda nb, T=T: X1[:, T - 1, nb * 512:(nb + 1) * 512], f'x1_{T}')

    def a2_hook0():
        if not SKIPA:
            a2_fa([0])
            a2_fb([0], 1)
            a2_fa(groups[0])
        S.dma('pool', WG[:], d_wg[:, :, :, :], writes=['wg'], sem='c_wg')
        S.dma('pool', RP[:], d_rp[:, :, :, :], writes=['rp'], sem='c_rp')

    load_weights(WA, w_in_0, 3072, [(0, 1024), (1024, 1536), (1536, 2048)], WO, w_out_0, 8, hook=a2_hook0)
    if not SKIPA:
        a2_x([0], 1, False)
        a2_fb(groups[0], 0)
        a2_fa(groups[1])
        a2_x(groups[0], 0, True)
        a2_fb(groups[1], 1)
        for G in range(1, 6):
            nxt, ns = (groups[G + 1], (G + 1) % 2) if G + 1 < 6 else ([19], 0)
            a2_fs(nxt)
            a2_hook[0] = nxt
            a2_x(groups[G], G % 2, True)
            a2_p(groups[G - 1], (G - 1) % 2)
            a2_fb(nxt, ns)
            a2_o(groups[G - 1])
        a2_x([19], 0, False)
        a2_p(groups[5], 1)
        a2_o(groups[5])

    if stop_after == 'A2':
        dump_x1_and_finish()
        return finish(nc, S, A)

    S.barrier()
    A.reset(phase_mark)
    HB1 = A.alloc("HB1", [128, D], BF16)
    WA = A.alloc("WA", [128, 8, 2560], BF16)
    WO = A.alloc("WO", [128, 8, 1024], BF16)
    G1 = A.alloc("G1", [128, D], F32)
    MK = A.alloc("MK", [128, 3, 2, 128], BF16)
    COS = A.alloc("COS", [128, 18, 8], F32)
    SIN = A.alloc("SIN", [128, 18, 8], F32)
    ESK = A.alloc("ESK", [128, 16], F32)
    HT1 = [A.alloc("HT1", [128, 8, 128], BF16) for _ in range(2)]
    QKF = A.alloc("QKF", [128, 20, 64], F32)
    QKB = A.alloc("QKB", [128, 20, 64], BF16)
    RT = [A.alloc("RT", [128, 20, 8], F32) for _ in range(2)]
    KT = A.alloc("KT", [128, 2, 18 * 128], BF16)
    VV = A.alloc("VV", [128, 18, 4, 66], BF16)
    QT = [A.alloc("QT", [128, 8, 128], BF16) for _ in range(3)]
    SG = [A.alloc("SG", [128, D], BF16) for _ in range(3)]
    TGB = A.alloc("TGB", [128, 512], F32)
    PTG = [A.alloc("PTG", [128, 3, 4, 128], BF16) for _ in range(2)]
    ORAW = A.alloc("ORAW", [128, 16, 65], F32)
    DEN = A.alloc("DEN", [128, 16], F32)
    YB1 = A.alloc("YB1", [128, D], BF16)
    YT = A.alloc("YT", [128, 8, 128], BF16)
    OB = [A.alloc("OB", [128, D], F32)]

    S.dma('sp', G0[:], d_g1[:, :], writes=['g0'], sem='c_g0')
    S.dma('sp', G1[:], d_gf[:, :], writes=['g1'], sem='c_g1')
    S.dma('pool', MK[:], d_mk[:, :, :, :], writes=['mk'], sem='c_mk')
    S.dma('sp', COS[:], d_cos[:, :, :], writes=['cos'], sem='c_cos')
    S.dma('sp', SIN[:], d_sin[:, :, :], writes=['sin'], sem='c_sin')
    S.dma('sp', ESK[:], d_sink[:, :], writes=['esk'], sem='c_esk')
    S.act(ESK[:], ESK[:], AF.Exp, ['esk'], ['esk'])
    S.op('pool', lambda e: e.memset(VV[:, :, :, 64:65], 1.0), writes=['vones'])
    ring = {'ptb': 0, 'sc': 0, 'po': 0, 'pr': 0}

    def pool_bank(kind):
        base, cnt = {'sc': (0, 3), 'po': (3, 1), 'pr': (4, 2)}[kind]
        i = base + ring[kind] % cnt
        ring[kind] += 1
        return PB[i], f'pb{i}'

    b_st = {}

    def b1a_s(T):
        b_st[T] = norm_stats(X1[:, T - 1, :], f'x1_{T}')

    def b1a_h(T):
        st, stk = b_st[T]
        S.stt(HB1[:], X1[:, T - 1, :], st, G0[:], ALU.mult, ALU.mult, [f'x1_{T}', stk, 'g0'], ['hb1'])

    def b1a_n(T):
        b1a_s(T)
        b1a_h(T)

    def b1a_t(T):
        HTt, htk = HT1[T % 2], f'ht1{T % 2}'
        pt, ptk = next_pt()
        for c in range(8):
            S.tr(pt[:, c, :], HB1[:, c * 128:(c + 1) * 128], IDENT[:], ['hb1'], [ptk], signal=(c == 7))
        S.copy('dve', HTt[:, :, :], pt[:, :, :], [ptk], [htk])

    def b_proj(T, c0):
        HTt, htk = HT1[T % 2], f'ht1{T % 2}'
        p, pk = pool_bank('pr')
        for kc in range(8):
            S.mm(p[:, :], HTt[:, kc, :], WA[:, kc, c0:c0 + 512], kc == 0, kc == 7, [htk, f'WA{kc}_{c0 // 512}'], [pk])
        return p, pk

    def b1b_kv(T):
        pkv, pkvk = b_proj(T, 1024)
        S.copy('act', VV[:, T - 1, :, 0:64], pkv[:, 256:512].rearrange("p (g d) -> p g d", g=4),
               [pkvk], [f'vv{T}'])
        S.copy('dve', QKF[:, 16:20, :], pkv[:, 0:256].rearrange("p (g d) -> p g d", g=4), [pkvk], ['qkf'])

    def b1b_q(T, qb):
        pq, pqk = b_proj(T, qb * 512)
        for j in range(2):
            S.copy('act' if j else 'dve', QKF[:, qb * 8 + j:qb * 8 + 8:2, :],
                   pq[:, j * 256:(j + 1) * 256].rearrange("p (i d) -> p i d", i=4), [pqk], ['qkf'])

    def b1b_g(T, gb):
        sg, sgk = SG[T % 3], f'sg{T % 3}'
        pg, pgk = b_proj(T, 1536 + gb * 512)
        S.act(TGB[:], pg[:, :], AF.Tanh, [pgk], ['tgb'], scale=0.5)
        S.stt(sg[:, gb * 512:(gb + 1) * 512], TGB[:], 1.0, pg[:, :], ALU.add, ALU.mult, ['tgb', pgk], [sgk])

    def b1b(T):
        b1b_kv(T)
        if 2 <= T <= 17:
            b1b_q(T, 0)
            b1b_q(T, 1)
            b1b_g(T, 0)
            b1b_g(T, 1)

    def b1c_rope(T):
        own = 2 <= T <= 17
        h0, nh = (0, 20) if own else (16, 4)
        hs = slice(h0, h0 + nh)
        cosb = COS[:, T - 1:T, :].broadcast_to([128, nh, 8])
        sinb = SIN[:, T - 1:T, :].broadcast_to([128, nh, 8])
        xa, xb = QKF[:, hs, 0:8], QKF[:, hs, 8:16]
        r = [RT[i][:, hs, :] for i in range(2)]
        S.tt('dve', r[0], xa, cosb, ALU.mult, ['qkf', 'cos'], ['rt0'])
        S.tt('dve', r[1], xb, sinb, ALU.mult, ['qkf', 'sin'], ['rt1'])
        S.tt('dve', QKB[:, hs, 0:8], r[0], r[1], ALU.subtract, ['rt0', 'rt1'], ['qkb_rot'])
        S.tt('dve', r[0], xb, cosb, ALU.mult, ['qkf', 'cos'], ['rt0'])
        S.tt('dve', r[1], xa, sinb, ALU.mult, ['qkf', 'sin'], ['rt1'])
        S.tt('dve', QKB[:, hs, 8:16], r[0], r[1], ALU.add, ['rt0', 'rt1'], ['qkb_rot'])
        S.copy('dve', QKB[:, hs, 16:64], QKF[:, hs, 16:64], ['qkf'], ['qkb'])

    def b1c_tr(T):
        own = 2 <= T <= 17
        pt, ptk = next_pt()
        for ch in range(2):
            S.tr(pt[:, ch, :], QKB[:, 16 + 2 * ch:18 + 2 * ch, :].rearrange("p h d -> p (h d)"), IDENT[:],
                 ['qkb', 'qkb_rot'], [ptk], signal=(ch == 1))
        S.copy('act', KT[:, :, (T - 1) * 128:T * 128], pt[:, 0:2, :], [ptk], [f'kt{T}'])
        if own:
            pt, ptk = next_pt()
            for c8 in range(8):
                S.tr(pt[:, c8, :], QKB[:, 2 * c8:2 * c8 + 2, :].rearrange("p h d -> p (h d)"), IDENT[:],
                     ['qkb', 'qkb_rot'], [ptk], signal=(c8 == 7))
            S.copy('dve', QT[T % 3][:, :, :], pt[:, :, :], [ptk], [f'qt{T % 3}'])

    def b2s(Tq, g):
        qt, qtk = QT[Tq % 3], f'qt{Tq % 3}'
        mk = MK[:, 1] if Tq == 2 else (MK[:, 2] if Tq == 17 else MK[:, 0])
        ch, pb = g // 2, (g % 2) * 64
        c0 = 4 if g >= 2 else 0
        pi = ring['ptb'] % 2
        ring['ptb'] += 1
        ptg, ptgk = PTG[pi], f'ptg{pi}'
        for j in (1, 0, 2):
            Tk = Tq - 1 + j
            ps, psk = pool_bank('sc')
            S.mm(ps[:, 0:512], KT[pb:pb + 64, ch, (Tk - 1) * 128:Tk * 128],
                 qt[pb:pb + 64, c0:c0 + 4, :].rearrange("p c q -> p (c q)"), True, True,
                 [f'kt{Tk}', qtk], [psk])
            S.act(ptg[:, j, :, :].rearrange("p h q -> p (h q)"), ps[:, 0:512], AF.Exp, [psk], [ptgk],
                  scale=0.125)
        S.tt('dve', ptg[:, 0:3:2, :, :], ptg[:, 0:3:2, :, :], mk.unsqueeze(2).broadcast_to([128, 2, 4, 128]),
             ALU.mult, [ptgk, 'mk'], [ptgk])
        return ptg, ptgk

    def b2v(Tq, g, ptg, ptgk):
        po, pok = pool_bank('po')
        for hh in range(4):
            for j in range(3):
                Tk = Tq - 1 + j
                S.mm(po[:, hh * 65:(hh + 1) * 65], ptg[:, j, hh, :], VV[:, Tk - 1, g, 0:65], j == 0, j == 2,
                     [ptgk, f'vv{Tk}', 'vones'], [pok], signal=(j == 2))
        S.copy('act', ORAW[:, g * 4:(g + 1) * 4, :], po[:, 0:260].rearrange("p (h d) -> p h d", h=4),
               [pok], ['oraw'])

    def b2f_a(Tq):
        sg, sgk = SG[Tq % 3], f'sg{Tq % 3}'
        S.tt('dve', DEN[:], ORAW[:, :, 64], ESK[:], ALU.add, ['oraw', 'esk'], ['den'])
        S.op('dve', lambda e: e.reciprocal(out=DEN[:], in_=DEN[:]), ['den'], ['den'])
        S.tt('dve', ORAW[:, :, 0:64], ORAW[:, :, 0:64], DEN[:].unsqueeze(2).broadcast_to([128, 16, 64]),
             ALU.mult, ['oraw', 'den'], ['oraw'])
        S.stt(YB1[:].rearrange("p (h d) -> p h d", h=16), ORAW[:, :, 0:64], 0.5,
              sg[:].rearrange("p (h d) -> p h d", h=16), ALU.mult, ALU.mult, ['oraw', sgk], ['yb1'])

    def b2f_b1(Tq):
        pt, ptk = next_pt()
        for c in range(8):
            S.tr(pt[:, c, :], YB1[:, c * 128:(c + 1) * 128], IDENT[:], ['yb1'], [ptk], signal=(c == 7))
        S.copy('act', YT[:, :, :], pt[:, :, :], [ptk], ['yt'])

    def b2f_o(Tq, nb):
        ob, obk = OB[0], 'ob0'
        pf, pfk = pool_bank('pr')
        for c in range(8):
            S.mm(pf[:, :], YT[:, c, :], WO[:, c, nb * 512:(nb + 1) * 512], c == 0, c == 7,
                 ['yt', f'WO{c}'], [pfk])
        S.tt('dve', ob[:, nb * 512:(nb + 1) * 512], pf[:, :], X1[:, Tq - 1, nb * 512:(nb + 1) * 512],
             ALU.add, [pfk, f'x1_{Tq}'], [obk])
        if nb == 1:
            b_st['f'] = norm_stats(ob[:], obk)

    def b2f_n(Tq):
        ob, obk = OB[0], 'ob0'
        st, stk = b_st['f']
        S.stt(ob[:], ob[:], st, G1[:], ALU.mult, ALU.mult, [obk, stk, 'g1'], [obk])
        S.dma('sp', out[(Tq - 2) * 128:(Tq - 1) * 128, :], ob[:], reads=[obk], writes=[f'out{Tq}'],
              sem='out0')

    load_weights(WA, w_in_1, 0, [(1024, 1536), (0, 512), (512, 1024), (1536, 2048), (2048, 2560)], WO, w_out_1, 0,
                 hook=lambda: (b1a_s(1), b1a_s(2)))
    b1a_h(1)
    b1a_t(1)
    b1b_kv(1)
    b1a_s(3)
    b1a_h(2)
    b1c_rope(1)
    b1a_t(2)
    b1b_kv(2)
    b1c_tr(1)
    b1b_q(2, 0)
    b1b_q(2, 1)
    b1a_h(3)
    b1c_rope(2)
    b1b_g(2, 0)
    b1a_t(3)
    b1b_g(2, 1)
    b1b_kv(3)
    b1c_tr(2)
    b1a_s(4)
    b1b_q(3, 0)
    b1b_q(3, 1)
    b1c_rope(3)
    b1b_g(3, 0)
    b1b_g(3, 1)
    b1a_h(4)
    b1c_tr(3)
    for Tq in range(2, 18):
        T2 = Tq + 2
        nxt = T2 <= 18
        own2 = nxt and T2 <= 17
        prev = Tq > 2
        p0 = b2s(Tq, 0)
        if nxt:
            b1a_t(T2)
        p1 = b2s(Tq, 1)
        b2v(Tq, 0, *p0)
        if T2 + 1 <= 18:
            b1a_s(T2 + 1)
        if nxt:
            b1b_kv(T2)
        if prev:
            b2f_b1(Tq - 1)
        p2 = b2s(Tq, 2)
        b2v(Tq, 1, *p1)
        if prev:
            b2f_o(Tq - 1, 0)
        if own2:
            b1b_q(T2, 0)
        p3 = b2s(Tq, 3)
        b2v(Tq, 2, *p2)
        if T2 + 1 <= 18:
            b1a_h(T2 + 1)
        if prev:
            b2f_o(Tq - 1, 1)
        if own2:
            b1b_q(T2, 1)
        if nxt:
            b1c_rope(T2)
        b2v(Tq, 3, *p3)
        if own2:
            b1b_g(T2, 0)
        if prev:
            b2f_n(Tq - 1)
        if own2:
            b1b_g(T2, 1)
        b2f_a(Tq)
        if nxt:
            b1c_tr(T2)
    b2f_b1(17)
    b2f_o(17, 0)
    b2f_o(17, 1)
    b2f_n(17)
    S.wait_all('sp', [f'out{Tq}' for Tq in range(2, 18)])
    return finish(nc, S, A)


def finish(nc, S, A):
    with nc.Block() as block:
        S.emit(block)
    return nc


def _pool_mats(first_edge, last_edge):
    R = np.zeros((128, 5, 4, 128), np.float32)
    tt = np.arange(128)
    for g, w in enumerate((2, 4, 8, 16)):
        hw = w // 2
        for blk in range(5):
            off = {0: -128, 1: 0, 2: 128, 3: 0, 4: 0}[blk]
            tp = tt[:, None] + off
            t = tt[None, :]
            inwin = (tp >= t - hw) & (tp < t + hw)
            cnt = np.full((1, 128), float(w), np.float32)
            if blk == 3 and first_edge:
                cnt = (np.minimum(t + hw, 10 ** 9) - np.maximum(t - hw, 0)).astype(np.float32)
            if blk == 4 and last_edge:
                cnt = (np.minimum(t + hw, 128) - (t - hw)).astype(np.float32)
            m = inwin.astype(np.float32) / cnt
            if blk in (1, 3, 4):
                m = m - np.eye(128, dtype=np.float32)
            R[:, blk, g, :] = m
    return R


def _prep(inputs):
    f = lambda a: np.ascontiguousarray(np.asarray(a, dtype=np.float32))
    x = f(inputs["x"])
    common = {
        "w_in_0": f(inputs["w_in_0"]), "w_out_0": f(inputs["w_out_0"]),
        "w_in_1": f(inputs["w_in_1"]), "w_out_1": f(inputs["w_out_1"]),
        "g0_bc": f(np.broadcast_to(f(inputs["norm_0"])[None, :], (128, D))),
        "g1_bc": f(np.broadcast_to(f(inputs["norm_1"])[None, :], (128, D))),
        "gf_bc": f(np.broadcast_to(f(inputs["final_norm"])[None, :], (128, D))),
        "ident": np.eye(128, dtype=np.float32),
        "wsT": f(f(inputs["a_spatial_w_0"]).transpose(2, 0, 1)),
        "b_bc": f(np.broadcast_to(f(inputs["a_spatial_b_0"])[None, :, :], (128, 4, 128))),
        "gv_col": f(f(inputs["a_v_norm_0"]).reshape(8, 128).T),
        "bsc_col": f(f(inputs["b_scale_0"]).reshape(8, 128).T),
        "wg": f(f(inputs["b_group_w_0"]).reshape(4, 2, 128, 256).transpose(2, 0, 1, 3)),
        "sink_bc": f(np.broadcast_to(f(inputs["sink_1"])[None, :], (128, 16))),
    }
    kk = np.arange(128)[:, None]
    qq = np.arange(128)[None, :]
    maskP = (kk >= qq).astype(np.float32)
    maskN = (kk <= qq).astype(np.float32)
    inv = (500000.0 ** (-np.arange(0, 16, 2, dtype=np.float32) / 16.0)).astype(np.float32)
    maps = []
    for c in range(N_CORES):
        b, q = c // 4, c % 4
        start = q * 2048 - 256
        xpad = np.zeros((NT * 128, D), np.float32)
        lo, hi = max(start, 0), min(start + NT * 128, S_LEN)
        xpad[lo - start:hi - start] = x[b, lo:hi]
        masks = np.zeros((128, 3, 2, 128), np.float32)
        masks[:, 0, 0], masks[:, 0, 1] = maskP, maskN
        masks[:, 1, 0], masks[:, 1, 1] = (0.0 if q == 0 else maskP), maskN
        masks[:, 2, 0], masks[:, 2, 1] = maskP, (0.0 if q == 3 else maskN)
        pos = (start + 128 + np.arange(18 * 128)).astype(np.float32)
        ang = pos[:, None] * inv[None, :]
        cos = np.cos(ang).astype(np.float32).reshape(18, 128, 8).transpose(1, 0, 2)
        sin = np.sin(ang).astype(np.float32).reshape(18, 128, 8).transpose(1, 0, 2)
        m = dict(common)
        m.update({"xp": xpad, "masks": masks, "cos": f(cos), "sin": f(sin),
                  "rpool": _pool_mats(q == 0, q == 3)})
        maps.append(m)
    return maps


_NC_CACHE = {}


def kernel(**inputs):
    maps = _prep(inputs)
    if "nc" not in _NC_CACHE:
        _NC_CACHE["nc"] = build()
    nc = _NC_CACHE["nc"]
    res = run_bass_kernel_spmd(nc, maps, core_ids=list(range(N_CORES)))
    outp = np.empty((2, S_LEN, D), np.float32)
    for c in range(N_CORES):
        b, q = c // 4, c % 4
        outp[b, q * 2048:(q + 1) * 2048] = res.results[c]["out"]
    return outp
```
